# Optimizing a Trainium2 kernel written in Bass

```python
import jax, jax.numpy as jnp
from jax import lax
import numpy as np

D_MODEL = 4096
BATCH = 2
SEQ = 4096
DEPTH = 2

N_MIXERS = 2
HEAD_DIM = 64
N_Q_HEADS = D_MODEL // HEAD_DIM
N_KV_HEADS = N_Q_HEADS // 8
GROUP = N_Q_HEADS // N_KV_HEADS
ATTN_INNER = N_Q_HEADS * HEAD_DIM
KV_DIM = N_KV_HEADS * HEAD_DIM
ATTN_IN_COLS = 2 * ATTN_INNER + 2 * KV_DIM
WINDOW = 128
BLOCK = 128
CONV_INNER = D_MODEL
CONV_WIDTH = 31
CONV_IN_COLS = 3 * CONV_INNER
N_ATTN_LAYERS = (DEPTH + N_MIXERS - 1) // N_MIXERS
N_CONV_LAYERS = DEPTH // N_MIXERS
EPS = 1e-5

kernel_name = "hybrid_swa_sink_conformer_conv"


def rms_norm(x, g):
    xf = x.astype(jnp.float32)
    xf = xf * lax.rsqrt(jnp.mean(xf * xf, axis=-1, keepdims=True) + EPS)
    return (xf * g.astype(jnp.float32)).astype(x.dtype)


def layer_norm(x, g, b):
    xf = x.astype(jnp.float32)
    mu = jnp.mean(xf, axis=-1, keepdims=True)
    var = jnp.mean(jnp.square(xf - mu), axis=-1, keepdims=True)
    y = (xf - mu) * lax.rsqrt(var + EPS) * g.astype(jnp.float32) + b.astype(jnp.float32)
    return y.astype(x.dtype)


def banded_sink_attention(q, k, v, sinks):
    B, S = q.shape[0], q.shape[1]
    nb = S // BLOCK
    qb = q.reshape(B, nb, BLOCK, N_KV_HEADS, GROUP, HEAD_DIM)
    kb = k.reshape(B, nb, BLOCK, N_KV_HEADS, HEAD_DIM)
    vb = v.reshape(B, nb, BLOCK, N_KV_HEADS, HEAD_DIM)
    kk = jnp.concatenate([jnp.concatenate([jnp.zeros_like(kb[:, :1]), kb[:, :-1]], axis=1), kb], axis=2)
    vv = jnp.concatenate([jnp.concatenate([jnp.zeros_like(vb[:, :1]), vb[:, :-1]], axis=1), vb], axis=2)
    scale = HEAD_DIM ** -0.5
    s = jnp.einsum('bnqkgd,bnskd->bnkgqs', qb, kk).astype(jnp.float32) * scale
    qi = jnp.arange(BLOCK)[:, None]
    kj = jnp.arange(2 * BLOCK)[None, :]
    diff = BLOCK + qi - kj
    band = (diff >= 0) & (diff < WINDOW)
    valid_prev = (jnp.arange(nb)[:, None, None] > 0) | (kj >= BLOCK)[None]
    mask = band[None] & valid_prev
    s = jnp.where(mask[None, :, None, None], s, -jnp.inf)
    sink = sinks.astype(jnp.float32).reshape(N_KV_HEADS, GROUP)[None, None, :, :, None, None]
    m = jnp.maximum(jnp.max(s, axis=-1, keepdims=True), sink)
    e = jnp.exp(s - m)
    p = e / (jnp.sum(e, axis=-1, keepdims=True) + jnp.exp(sink - m))
    o = jnp.einsum('bnkgqs,bnskd->bnqkgd', p.astype(v.dtype), vv)
    return o.reshape(B, S, ATTN_INNER)


def attention_mixer(h, w_in, b_in, sinks, w_out, b_out):
    B, S, _ = h.shape
    proj = h @ w_in + b_in
    q, k, v, z = jnp.split(proj, [ATTN_INNER, ATTN_INNER + KV_DIM, ATTN_INNER + 2 * KV_DIM], axis=-1)
    q = q.reshape(B, S, N_Q_HEADS, HEAD_DIM)
    k = k.reshape(B, S, N_KV_HEADS, HEAD_DIM)
    v = v.reshape(B, S, N_KV_HEADS, HEAD_DIM)
    o = banded_sink_attention(q, k, v, sinks)
    return (o * jax.nn.silu(z)) @ w_out + b_out


def conformer_conv_mixer(h, w_in, b_in, dw_w, dw_b, ln_g, ln_b, w_out, b_out):
    proj = h @ w_in + b_in
    a, g, z = jnp.split(proj, 3, axis=-1)
    u = a * jax.nn.sigmoid(g)
    u = lax.conv_general_dilated(
        u, dw_w[:, None, :].astype(u.dtype), window_strides=(1,),
        padding=[(CONV_WIDTH - 1, 0)], dimension_numbers=('NWC', 'WIO', 'NWC'),
        feature_group_count=CONV_INNER) + dw_b
    u = jax.nn.silu(layer_norm(u, ln_g, ln_b))
    return (u * jax.nn.silu(z)) @ w_out + b_out


def setup_inputs(seed: int = 0) -> dict:
    key = jax.random.key(seed)
    ks = jax.random.split(key, 18)
    nrm = jax.random.normal
    NA, NC = N_ATTN_LAYERS, N_CONV_LAYERS
    return {
        "x": nrm(ks[0], (BATCH, SEQ, D_MODEL), jnp.float32),
        "norm_g": 1.0 + 0.01 * nrm(ks[1], (DEPTH, D_MODEL), jnp.float32),
        "attn_w_in": nrm(ks[2], (NA, D_MODEL, ATTN_IN_COLS), jnp.float32) * D_MODEL ** -0.5,
        "attn_b_in": 0.01 * nrm(ks[3], (NA, ATTN_IN_COLS), jnp.float32),
        "attn_sinks": 0.5 * nrm(ks[4], (NA, N_Q_HEADS), jnp.float32),
        "attn_w_out": nrm(ks[5], (NA, ATTN_INNER, D_MODEL), jnp.float32) * ATTN_INNER ** -0.5,
        "attn_b_out": 0.01 * nrm(ks[6], (NA, D_MODEL), jnp.float32),
        "conv_w_in": nrm(ks[7], (NC, D_MODEL, CONV_IN_COLS), jnp.float32) * D_MODEL ** -0.5,
        "conv_b_in": 0.01 * nrm(ks[8], (NC, CONV_IN_COLS), jnp.float32),
        "conv_dw_w": nrm(ks[9], (NC, CONV_WIDTH, CONV_INNER), jnp.float32) * CONV_WIDTH ** -0.5,
        "conv_dw_b": 0.01 * nrm(ks[10], (NC, CONV_INNER), jnp.float32),
        "conv_ln_g": 1.0 + 0.01 * nrm(ks[11], (NC, CONV_INNER), jnp.float32),
        "conv_ln_b": 0.01 * nrm(ks[12], (NC, CONV_INNER), jnp.float32),
        "conv_w_out": nrm(ks[13], (NC, CONV_INNER, D_MODEL), jnp.float32) * CONV_INNER ** -0.5,
        "conv_b_out": 0.01 * nrm(ks[14], (NC, D_MODEL), jnp.float32),
        "final_g": 1.0 + 0.01 * nrm(ks[15], (D_MODEL,), jnp.float32),
    }


def reference(x, norm_g, attn_w_in, attn_b_in, attn_sinks, attn_w_out, attn_b_out,
              conv_w_in, conv_b_in, conv_dw_w, conv_dw_b, conv_ln_g, conv_ln_b,
              conv_w_out, conv_b_out, final_g):
    for i in range(DEPTH):
        h = rms_norm(x, norm_g[i])
        j = i // N_MIXERS
        if i % N_MIXERS == 0:
            x = x + attention_mixer(h, attn_w_in[j], attn_b_in[j], attn_sinks[j],
                                    attn_w_out[j], attn_b_out[j])
        else:
            x = x + conformer_conv_mixer(h, conv_w_in[j], conv_b_in[j], conv_dw_w[j], conv_dw_b[j],
                                         conv_ln_g[j], conv_ln_b[j], conv_w_out[j], conv_b_out[j])
    return rms_norm(x, final_g)
```

```python
import contextlib
import numpy as np
import ml_dtypes
import concourse.bass as bass
import concourse.mybir as mybir
from concourse.bass_utils import run_bass_kernel_spmd

F32 = mybir.dt.float32
BF16 = mybir.dt.bfloat16
ALU = mybir.AluOpType
AF = mybir.ActivationFunctionType

D = 4096
KC = 32
OWN = 1024
HQ = 32
NQ = OWN + HQ
HK = 160
NK = OWN + HK
NCORES = 8
EPS = 1e-5
CW = 31

QCH = [(0, 352), (352, 352), (704, 352)]
OCH = [(0, 512), (512, 512)]


def _param_layout():
    off = {}
    cur = 0
    for name, n in [("g0", 32), ("g1", 32), ("fg", 32), ("bq", 32), ("bk", 8), ("bz", 32),
                    ("bo0", 32), ("ba", 32), ("bg", 32), ("bz1", 32), ("dww", 32 * CW),
                    ("dwb", 32), ("lng", 32), ("lnb", 32), ("bo1", 32), ("sinks", 64),
                    ("flag", 1), ("bv", 512)]:
        off[name] = (cur, n)
        cur += n
    return off, cur


POFF, NPAR = _param_layout()


def _pcol(v):
    return np.ascontiguousarray(np.asarray(v, np.float32).reshape(-1, 128).T)


class Sem:
    def __init__(self, h):
        self.h = h
        self.n = 0


class Buf:
    def __init__(self, name):
        self.name = name
        self.w = {}
        self.r = {}
        self.alias = []


class Prog:
    def __init__(self, nc, stack):
        self.nc = nc
        self.stack = stack
        self.q = {k: [] for k in ("pe", "act", "dve", "pool", "sp")}
        self.waited = {k: {} for k in self.q}
        self.nsem = 0
        self.esem = {k: self.sem("e_" + k) for k in ("pe", "act", "dve", "pool")}
        self.pend_r = set()
        self.pend_w = set()

    def sem(self, name):
        self.nsem += 1
        return Sem(self.stack.enter_context(self.nc.semaphore("%s_%d" % (name, self.nsem))))

    def _deps(self, eng, reads, writes):
        need = {}
        for b in reads:
            for s, v in b.w.items():
                need[s] = max(need.get(s, 0), v)
        for b in writes:
            for bb in [b] + b.alias:
                for s, v in bb.w.items():
                    need[s] = max(need.get(s, 0), v)
                for s, v in bb.r.items():
                    need[s] = max(need.get(s, 0), v)
        out = []
        wd = self.waited[eng]
        for s, v in need.items():
            if wd.get(s, 0) < v:
                wd[s] = v
                out.append((s.h, v))
        return out

    @staticmethod
    def _mark(ev, reads, writes):
        s, v = ev
        for b in writes:
            b.w = {s: v}
            b.r = {}
        for b in reads:
            if b in writes:
                continue
            b.r[s] = max(b.r.get(s, 0), v)

    def op(self, eng, fn, reads=(), writes=()):
        waits = self._deps(eng, reads, writes)
        s = self.esem[eng]
        s.n += 1
        ev = (s, s.n)

        def run(e, waits=waits, fn=fn, h=s.h):
            for hh, v in waits:
                e.wait_ge(hh, v)
            fn(e).then_inc(h, 1)
        self.q[eng].append(run)
        self._mark(ev, reads, writes)
        return ev

    def pe(self, fn, reads=(), writes=(), sig=False):
        waits = self._deps("pe", reads, writes)
        self.pend_r |= set(reads)
        self.pend_w |= set(writes)
        s = self.esem["pe"]
        if sig:
            s.n += 1
            ev = (s, s.n)
            self._mark(ev, [b for b in self.pend_r if b not in self.pend_w], list(self.pend_w))
            self.pend_r = set()
            self.pend_w = set()

        def run(e, waits=waits, fn=fn, h=s.h, sig=sig):
            for hh, v in waits:
                e.wait_ge(hh, v)
            ins = fn(e)
            if sig:
                ins.then_inc(h, 1)
        self.q["pe"].append(run)

    def dma(self, eng, sem, out, in_, reads=(), writes=(), n=1):
        waits = self._deps(eng, reads, writes)
        sem.n += 16
        ev = (sem, sem.n)

        def run(e, waits=waits, out=out, in_=in_, h=sem.h):
            for hh, v in waits:
                e.wait_ge(hh, v)
            e.dma_start(out=out, in_=in_).then_inc(h, 16)
        self.q[eng].append(run)
        self._mark(ev, reads, writes)
        return ev

    def wait_all(self, eng, bufs):
        waits = self._deps(eng, bufs, bufs)

        def run(e, waits=waits):
            for hh, v in waits:
                e.wait_ge(hh, v)
        self.q[eng].append(run)


class Arena:
    def __init__(self, t, nbytes):
        self.t = t
        self.nbytes = nbytes

    def view(self, off, shape, dt):
        assert off % 32 == 0
        esz = 4 if dt == F32 else 2
        n = int(np.prod(shape))
        assert off + n * esz <= self.nbytes, (off, n * esz, self.nbytes)
        v = self.t[:, off // 4: off // 4 + (n * esz + 3) // 4]
        if dt != F32:
            v = v.bitcast(dt)[:, 0:n]
        if len(shape) == 2:
            v = v.rearrange("p (a b) -> p a b", a=shape[0])
        elif len(shape) == 3:
            v = v.rearrange("p (a b c) -> p a b c", a=shape[0], b=shape[1])
        return v


def _al(x):
    return (x + 31) // 32 * 32


def build(mode="fused", npass=1):
    do_l0 = mode in ("fused", "l0")
    do_l1 = mode in ("fused", "l1")
    nc = bass.Bass("TRN2", target_bir_lowering=False)

    def din(name, shape, dt=F32):
        return nc.dram_tensor(name, list(shape), dt, kind="ExternalInput")

    if do_l0:
        w_in0 = din("attn_w_in", [D, 9216]).ap().rearrange("(kc p) n -> p kc n", p=128)
        w_out0 = din("attn_w_out", [D, D]).ap().rearrange("(kc p) n -> p kc n", p=128)
    if do_l1:
        w_in1 = din("conv_w_in", [D, 12288]).ap().rearrange("(kc p) n -> p kc n", p=128)
        w_out1 = din("conv_w_out", [D, D]).ap().rearrange("(kc p) n -> p kc n", p=128)
    x1kind = {"fused": "Internal", "l0": "ExternalOutput", "l1": "ExternalInput"}[mode]

    o = 0
    O_PRM = o; o = _al(o + NPAR * 4)
    O_ONES = o; o = _al(o + 128 * 4)
    O_MSK = o; o = _al(o + 4 * 128 * 2)
    O_SINK = o; o = _al(o + 64 * 4)
    O_RSTD = o; o = _al(o + NK * 4)
    O_ACC = o; o = _al(o + NK * 4)
    O_R1 = o; o = _al(o + KC * NQ * 2)
    O_R2 = o; o = _al(o + KC * OWN * 2)
    NW = 2
    O_W = o; o = _al(o + NW * KC * 128 * 2)
    O_R4 = o; o = _al(o + 37632)
    TOTAL = o
    assert TOTAL <= 212000, TOTAL

    with contextlib.ExitStack() as stack:
        arena_t = stack.enter_context(nc.sbuf_tensor("arena", [128, TOTAL // 4], F32))
        banks = [stack.enter_context(nc.psum_tensor("bank%d" % i, [128, 512], F32)) for i in range(8)]
        A = Arena(arena_t, TOTAL)
        P = Prog(nc, stack)
        bankB = [Buf("bank%d" % i) for i in range(8)]

        prm = A.view(O_PRM, [NPAR], F32)
        ones = A.view(O_ONES, [128], F32)
        msk = A.view(O_MSK, [4, 128], BF16)
        sinkexp = A.view(O_SINK, [64], F32)
        rstd = A.view(O_RSTD, [NK], F32)
        acc = A.view(O_ACC, [NK], F32)
        R1 = A.view(O_R1, [KC, NQ], BF16)
        wslot = [A.view(O_W + i * KC * 128 * 2, [KC, 128], BF16) for i in range(NW)]

        def pc(name, j=0, n=1, rows=slice(0, 128)):
            o0, _ = POFF[name]
            return prm[rows, o0 + j: o0 + j + n]

        bPRM, bONES, bMSK, bSINK, bRSTD, bACC = (Buf(n) for n in
                                                  ("prm", "ones", "msk", "sink", "rstd", "acc"))
        bR1 = [Buf("R1_%d" % k) for k in range(KC)]
        r1b = lambda kc: [bR1[kc]]
        bW = [Buf("w%d" % i) for i in range(NW)]
        r4_all = []
        r2_all = []

        def phase(bufs, pool):
            for b in bufs:
                b.alias = list(pool)
            pool.extend(bufs)
        sW = [P.sem("wld%d" % i) for i in range(NW)]
        sLD = [P.sem("ld%d" % i) for i in range(3)]
        sSTO = [P.sem("sto%d" % i) for i in range(2)]
        sPRM = [P.sem("prm"), P.sem("msk")]
        sWv = P.sem("wv")
        sGL = P.sem("gld")
        wctr = [0]

        P.op("dve", lambda e: e.memset(ones, 1.0), writes=[bONES])

        def load_w(src_ap_list):
            i = wctr[0] % NW
            wctr[0] += 1
            for (dst_sl, src) in src_ap_list:
                P.dma("pool", sW[i], wslot[i][:, :, dst_sl], src, writes=[bW[i]])
            return i

        def formA(wi, act_chunks, out_banks, act_bufs):
            nch = len(act_chunks)
            for kc in range(KC):
                for c in range(nch):
                    apf, n = act_chunks[c]
                    bk = out_banks[c]
                    P.pe(lambda e, bk=bk, n=n, kc=kc, apf=apf: e.matmul(
                        banks[bk][:, 0:n], lhsT=wslot[wi][:, kc, :], rhs=apf(kc),
                        start=(kc == 0), stop=(kc == KC - 1)),
                        reads=[bW[wi]] + act_bufs(kc), writes=[bankB[bk]], sig=(kc == KC - 1))

        def bcast_stat(src_ap, n_tot, chunks, fn_evac):
            for ci, (c0, n) in enumerate(chunks):
                P.pe(lambda e, ci=ci, c0=c0, n=n: e.matmul(
                    banks[ci][:, 0:n], lhsT=ones, rhs=src_ap[:, c0:c0 + n], start=True, stop=True),
                    reads=[bONES, bACC], writes=[bankB[ci]], sig=True)
                fn_evac(ci, c0, n)

        def rstd_from_sumsq(n_tot, chunks):
            def ev(ci, c0, n):
                P.op("act", lambda e: e.activation(out=rstd[:, c0:c0 + n], in_=banks[ci][:, 0:n],
                                                   func=AF.Sqrt, bias=pc_eps, scale=1.0 / D),
                     reads=[bankB[ci], bEPS], writes=[bRSTD])
                P.op("dve", lambda e: e.reciprocal(out=rstd[:, c0:c0 + n], in_=rstd[:, c0:c0 + n]),
                     reads=[bRSTD], writes=[bRSTD])
            bcast_stat(acc, n_tot, chunks, ev)

        eps_t = stack.enter_context(nc.sbuf_tensor("epsc", [128, 1], F32))
        pc_eps = eps_t[:, 0:1]
        bEPS = Buf("eps")
        P.op("dve", lambda e: e.memset(eps_t[:], EPS), writes=[bEPS])

        def emit_pass(ps):
            sfx = "" if npass == 1 else "_%d" % ps
            prm_d = din("params" + sfx, [128, NPAR]).ap()
            msk_d = din("masks" + sfx, [128, 4, 128], BF16).ap()
            P.dma("sp", sPRM[0], prm, prm_d, writes=[bPRM])
            P.dma("sp", sPRM[1], msk, msk_d, writes=[bMSK])
            if do_l0:
                xT_d = din("xT" + sfx, [KC, 128, NK])
                g_d = nc.dram_tensor("gsc" + sfx, [KC, 128, NQ], BF16, kind="Internal")
            if do_l1:
                y_d = nc.dram_tensor("yT" + sfx, [KC, 128, OWN], F32, kind="ExternalOutput")
            x1_d = nc.dram_tensor("x1sc" + sfx, [KC, 128, NQ], F32, kind=x1kind)
            if do_l0:
                hT = R1
                KT = A.view(O_R2, [8, NK], BF16)
                Vaug = A.view(O_R2 + 18944, [10, 1088], BF16)
                QT = [A.view(O_R2 + 18944 + 21760 + i * 8448, [4, NQ], BF16) for i in range(2)]
                Wv = A.view(O_R2 + 18944 + 21760, [KC, 256], BF16)
                bKT, bV, bWv = Buf("KT"), Buf("Vaug"), Buf("Wv")
                bQT = [Buf("QT0"), Buf("QT1")]
                xs = [A.view(O_R4 + i * 4736, [NK], F32) for i in range(3)]
                bXS = [Buf("xs%d" % i) for i in range(3)]
                sXS = sLD
                hTe = A.view(O_R4 + 14208, [KC, HK], BF16)
                bHTe = Buf("hTe")
                sq_t = A.view(O_R4 + 14208 + 10240, [NK], F32)
                bSQ = Buf("sq")
                phase(bXS + [bHTe, bSQ], r4_all)
                phase([bKT, bV, bWv], r2_all)
                phase(bQT, r2_all)
                sz = [A.view(O_R4 + i * 8448, [4, NQ], BF16) for i in range(2)]
                bSZ = [Buf("sz0"), Buf("sz1")]
                PT = [A.view(O_R4 + 16896 + i * 4096, [4, 512], BF16) for i in range(2)]
                bPT = [[Buf("pt%d_%d" % (i, s)) for s in range(4)] for i in range(2)]
                rb = [A.view(O_R4 + 25088 + i * 2048, [512], F32) for i in range(2)]
                bRB = [Buf("rb0"), Buf("rb1")]
                G = A.view(O_R4 + 29184, [4, NQ], BF16)
                bG = Buf("G")
                phase(bSZ + bPT[0] + bPT[1] + bRB + [bG], r4_all)
                sG = sSTO[0]

                P.op("act", lambda e: e.activation(out=sinkexp, in_=pc("sinks", 0, 64), func=AF.Exp),
                     reads=[bPRM], writes=[bSINK])

                xTv = xT_d.ap()
                for kc in range(KC):
                    i = kc % 3
                    P.dma("sp", sXS[i], xs[i], xTv[kc], writes=[bXS[i]])
                    if kc == 0:
                        P.op("dve", lambda e, i=i: e.tensor_tensor(out=acc, in0=xs[i], in1=xs[i], op=ALU.mult),
                             reads=[bXS[i]], writes=[bACC])
                    else:
                        P.op("act", lambda e, i=i: e.activation(out=sq_t, in_=xs[i], func=AF.Square),
                             reads=[bXS[i]], writes=[bSQ])
                        P.op("dve", lambda e: e.tensor_tensor(out=acc, in0=acc, in1=sq_t, op=ALU.add),
                             reads=[bSQ, bACC], writes=[bACC])
                rstd_from_sumsq(NK, [(0, 512), (512, 512), (1024, 160)])
                for kc in range(KC):
                    i = kc % 3
                    P.dma("sp", sXS[i], xs[i], xTv[kc], writes=[bXS[i]])
                    P.op("dve", lambda e, i=i, kc=kc: e.scalar_tensor_tensor(
                        out=hT[:, kc, :], in0=xs[i][:, 128:NK], scalar=pc("g0", kc), in1=rstd[:, 128:NK],
                        op0=ALU.mult, op1=ALU.mult), reads=[bXS[i], bRSTD, bPRM], writes=[bR1[kc]])
                    P.op("dve", lambda e, i=i, kc=kc: e.scalar_tensor_tensor(
                        out=hTe[:, kc, :], in0=xs[i][:, 0:HK], scalar=pc("g0", kc), in1=rstd[:, 0:HK],
                        op0=ALU.mult, op1=ALU.mult), reads=[bXS[i], bRSTD, bPRM], writes=[bHTe])

                kch = [(lambda kc: hTe[:, kc, 0:HK], HK),
                       (lambda kc: hT[:, kc, 32:544], 512),
                       (lambda kc: hT[:, kc, 544:1056], 512)]
                kdst = [(0, HK), (HK, 512), (HK + 512, 512)]
                for h in range(8):
                    c0 = 4096 + 64 * h
                    wi = load_w([(slice(0, 64), w_in0[:, :, c0:c0 + 64]), (slice(64, 128), w_in0[:, :, c0:c0 + 64])])
                    formA(wi, kch, [0, 1, 2], lambda kc: [bR1[kc], bHTe])
                    for c in range(3):
                        d0, n = kdst[c]
                        P.op("act", lambda e, c=c, d0=d0, n=n, h=h: e.activation(
                            out=KT[:, h, d0:d0 + n], in_=banks[c][:, 0:n], func=AF.Identity, bias=pc("bk", h)),
                            reads=[bankB[c], bPRM], writes=[bKT])

                P.op("dve", lambda e: e.memset(Vaug, 1.0), writes=[bV])
                vt = [(lambda kc: hTe[:, kc, 0:32], 32), (lambda kc: hTe[:, kc, 32:160], 128)]
                for n_ in range(8):
                    vt.append((lambda kc, n_=n_: hT[:, kc, 32 + 128 * n_: 160 + 128 * n_], 128))
                bvo = POFF["bv"][0]
                for hf in range(2):
                    c0 = 4608 + 256 * hf
                    P.dma("pool", sWv, Wv, w_in0[:, :, c0:c0 + 256], writes=[bWv])
                    for t in range(10):
                        apf, m = vt[t]
                        bk = 3 + (t % 2)
                        for kc in range(KC):
                            P.pe(lambda e, bk=bk, m=m, kc=kc, apf=apf: e.matmul(
                                banks[bk][0:m, 0:256], lhsT=apf(kc), rhs=Wv[:, kc, :],
                                start=(kc == 0), stop=(kc == KC - 1)),
                                reads=[bWv, bR1[kc], bHTe], writes=[bankB[bk]], sig=(kc == KC - 1))
                        P.op("dve", lambda e, bk=bk, m=m, t=t, hf=hf: e.tensor_tensor(
                            out=Vaug[0:m, t, 64 + 512 * hf: 64 + 512 * hf + 512].rearrange("p (k c) -> p k c", k=4)[:, :, 0:64],
                            in0=banks[bk][0:m, 0:256].rearrange("p (k c) -> p k c", k=4),
                            in1=prm[0:m, bvo + 256 * hf: bvo + 256 * hf + 256].rearrange("p (k c) -> p k c", k=4),
                            op=ALU.add), reads=[bankB[bk], bPRM], writes=[bV])

                qch = [(lambda kc, c0=c0, n=n: hT[:, kc, c0:c0 + n], n) for (c0, n) in QCH]
                sctr = [0]

                def attn_S(g, n):
                    qb = g % 2
                    pb = (n + 1) % 2
                    if n < 0:
                        qs, nq = 0, 32
                        keys = [(0, 32, msk[0:32, 3, 0:32]), (32, 128, msk[:, 0, 96:128])]
                    else:
                        qs, nq = 32 + 128 * n, 128
                        keys = [(32 + 128 * n, 128, msk[:, 2, :] if n == 0 else msk[:, 1, :]),
                                (160 + 128 * n, 128, msk[:, 0, :])]
                    for half in range(2):
                        rows = slice(64 * half, 64 * half + 64)
                        for kb in range(2):
                            k0, nk, m_ap = keys[kb]
                            slot = 2 * half + kb
                            sb = 3 + (sctr[0] % 2)
                            sctr[0] += 1
                            P.pe(lambda e, sb=sb, nk=nk, nq=nq, rows=rows, k0=k0, qs=qs, qb=qb, g=g: e.matmul(
                                banks[sb][0:nk, 0:4 * nq], lhsT=KT[rows, g, k0:k0 + nk],
                                rhs=QT[qb][rows, :, qs:qs + nq], start=True, stop=True),
                                reads=[bKT, bQT[qb]], writes=[bankB[sb]], sig=True)
                            P.op("act", lambda e, sb=sb, nk=nk, nq=nq, pb=pb, slot=slot: e.activation(
                                out=PT[pb][0:nk, slot, 0:4 * nq], in_=banks[sb][0:nk, 0:4 * nq],
                                func=AF.Exp, scale=0.125), reads=[bankB[sb]], writes=[bPT[pb][slot]])
                            P.op("dve", lambda e, nk=nk, nq=nq, pb=pb, slot=slot, m_ap=m_ap: e.tensor_tensor(
                                out=PT[pb][0:nk, slot, 0:4 * nq].rearrange("p (h q) -> p h q", h=4),
                                in0=PT[pb][0:nk, slot, 0:4 * nq].rearrange("p (h q) -> p h q", h=4),
                                in1=m_ap[:, None, :].to_broadcast([nk, 4, nq]), op=ALU.mult),
                                reads=[bPT[pb][slot], bMSK], writes=[bPT[pb][slot]])

                def attn_PV(g, n):
                    qb = g % 2
                    pb = (n + 1) % 2
                    if n < 0:
                        qs, nq = 0, 32
                        tiles = [(0, 32), (1, 128)]
                    else:
                        qs, nq = 32 + 128 * n, 128
                        tiles = [(1 + n, 128), (2 + n, 128)]
                    for half in range(2):
                        ob = 5 + half
                        vc0 = 64 + 128 * g if half == 0 else 128 * g
                        for kb in range(2):
                            t, nk = tiles[kb]
                            slot = 2 * half + kb
                            P.pe(lambda e, ob=ob, nq=nq, nk=nk, t=t, vc0=vc0, pb=pb, slot=slot, kb=kb: e.matmul(
                                banks[ob][:, 0:4 * nq], lhsT=Vaug[0:nk, t, vc0:vc0 + 128],
                                rhs=PT[pb][0:nk, slot, 0:4 * nq], start=(kb == 0), stop=(kb == 1)),
                                reads=[bV, bPT[pb][slot]], writes=[bankB[ob]], sig=(kb == 1))
                        num = slice(0, 64) if half == 0 else slice(64, 128)
                        den = slice(64, 128) if half == 0 else slice(0, 64)
                        so = POFF["sinks"][0]
                        r = rb[half]
                        w4 = 4 * nq
                        P.op("dve", lambda e, ob=ob, den=den, num=num, g=g, half=half, r=r, nq=nq, w4=w4: e.tensor_tensor(
                            out=r[num, 0:w4].rearrange("p (h q) -> p h q", h=4),
                            in0=banks[ob][den, 0:w4].rearrange("p (h q) -> p h q", h=4),
                            in1=sinkexp[den, 8 * g + 4 * half: 8 * g + 4 * half + 4][:, :, None].to_broadcast([64, 4, nq]),
                            op=ALU.add), reads=[bankB[ob], bSINK], writes=[bRB[half]])
                        P.op("dve", lambda e, r=r, num=num, w4=w4: e.reciprocal(out=r[num, 0:w4], in_=r[num, 0:w4]),
                             reads=[bRB[half]], writes=[bRB[half]])
                        P.op("dve", lambda e, r=r, num=num, w4=w4, qb=qb, qs=qs, nq=nq: e.tensor_tensor(
                            out=r[num, 0:w4].rearrange("p (h q) -> p h q", h=4),
                            in0=r[num, 0:w4].rearrange("p (h q) -> p h q", h=4),
                            in1=sz[qb][num, :, qs:qs + nq], op=ALU.mult),
                            reads=[bRB[half], bSZ[qb]], writes=[bRB[half]])
                        P.op("dve", lambda e, r=r, num=num, w4=w4, ob=ob, qs=qs, nq=nq: e.tensor_tensor(
                            out=G[num, :, qs:qs + nq],
                            in0=banks[ob][num, 0:w4].rearrange("p (h q) -> p h q", h=4),
                            in1=r[num, 0:w4].rearrange("p (h q) -> p h q", h=4), op=ALU.mult),
                            reads=[bankB[ob], bRB[half]], writes=[bG])

                gv = g_d.ap()
                bGS = [Buf("gsc%d" % g) for g in range(8)]

                def attn_steps(g):
                    L = [-1] + list(range(8))
                    slots = [[lambda: attn_S(g, L[0])]]
                    for i in range(8):
                        slots.append([lambda i=i: attn_PV(g, L[i]), lambda i=i: attn_S(g, L[i + 1])])
                    slots[-1].append(lambda: attn_PV(g, L[8]))

                    def fin():
                        for p in range(4):
                            P.dma("sp", sG, gv[4 * g + p], G[:, p, :], reads=[bG], writes=[bGS[g]])
                    slots[-1].append(fin)
                    return slots

                def inproj_block(g, i):
                    qb = g % 2
                    if i < 4:
                        c0 = 128 * (4 * g + i)
                        wi = load_w([(slice(0, 128), w_in0[:, :, c0:c0 + 128])])
                        formA(wi, qch, [0, 1, 2], r1b)
                        for c, (t0, n) in enumerate(QCH):
                            P.op("act", lambda e, c=c, t0=t0, n=n, i=i, qb=qb, g=g: e.activation(
                                out=QT[qb][:, i, t0:t0 + n], in_=banks[c][:, 0:n], func=AF.Identity,
                                bias=pc("bq", 4 * g + i)), reads=[bankB[c], bPRM], writes=[bQT[qb]])
                    else:
                        p = i - 4
                        c0 = 5120 + 128 * (4 * g + p)
                        wi = load_w([(slice(0, 128), w_in0[:, :, c0:c0 + 128])])
                        formA(wi, qch, [0, 1, 2], r1b)
                        for c, (t0, n) in enumerate(QCH):
                            P.op("act", lambda e, c=c, t0=t0, n=n, p=p, qb=qb, g=g: e.activation(
                                out=sz[qb][:, p, t0:t0 + n], in_=banks[c][:, 0:n], func=AF.Silu,
                                bias=pc("bz", 4 * g + p)), reads=[bankB[c], bPRM], writes=[bSZ[qb]])

                for g in range(9):
                    slots = attn_steps(g - 1) if g >= 1 else None
                    if slots is not None:
                        for f in slots[0]:
                            f()
                    for i in range(8):
                        if g < 8:
                            inproj_block(g, i)
                        if slots is not None:
                            for f in slots[i + 1]:
                                f()

                gT = R1
                for kc in range(KC):
                    P.dma("sp", sGL, gT[:, kc, :], gv[kc], reads=[bGS[kc // 4]], writes=[bR1[kc]])
                for kc in range(KC):
                    bR1[kc].w = {sGL: sGL.n}

                xres = [A.view(O_R4 + i * 4224, [NQ], F32) for i in range(2)]
                x1st = [A.view(O_R4 + 8448 + i * 4224, [NQ], F32) for i in range(2)]
                sq1 = A.view(O_R4 + 16896, [NQ], F32)
                bXR = [Buf("xr0"), Buf("xr1")]
                bX1 = [Buf("x1s0"), Buf("x1s1")]
                bSQ1 = Buf("sq1")
                phase(bXR + bX1 + [bSQ1], r4_all)
                sXR = sLD[0:2]
                sX1 = sSTO
                bX1D = [Buf("x1d%d" % i) for i in range(KC)]
                gch = [(lambda kc, c0=c0, n=n: gT[:, kc, c0:c0 + n], n) for (c0, n) in QCH]
                x1v = x1_d.ap()
                for ob in range(KC):
                    i = ob % 2
                    P.dma("sp", sXR[i], xres[i], xTv[ob][:, 128:NK], writes=[bXR[i]])
                    wi = load_w([(slice(0, 128), w_out0[:, :, 128 * ob:128 * ob + 128])])
                    formA(wi, gch, [0, 1, 2], r1b)
                    for c, (t0, n) in enumerate(QCH):
                        P.op("dve", lambda e, c=c, t0=t0, n=n, i=i, ob=ob: e.scalar_tensor_tensor(
                            out=x1st[i][:, t0:t0 + n], in0=banks[c][:, 0:n], scalar=pc("bo0", ob),
                            in1=xres[i][:, t0:t0 + n], op0=ALU.add, op1=ALU.add),
                            reads=[bankB[c], bXR[i], bPRM], writes=[bX1[i]])
                    P.dma("sp", sX1[i], x1v[ob], x1st[i], reads=[bX1[i]], writes=[bX1D[ob]])
                    if ob == 0:
                        P.op("dve", lambda e, i=i: e.tensor_tensor(out=acc[:, 0:NQ], in0=x1st[i], in1=x1st[i], op=ALU.mult),
                             reads=[bX1[i]], writes=[bACC])
                    else:
                        P.op("act", lambda e, i=i: e.activation(out=sq1, in_=x1st[i], func=AF.Square),
                             reads=[bX1[i]], writes=[bSQ1])
                        P.op("dve", lambda e: e.tensor_tensor(out=acc[:, 0:NQ], in0=acc[:, 0:NQ], in1=sq1, op=ALU.add),
                             reads=[bSQ1, bACC], writes=[bACC])
                if mode == "l0":
                    P.wait_all("sp", bX1D)

            if do_l1:
                x1v = x1_d.ap()
                if not do_l0:
                    bX1D = [Buf("x1d%d" % i) for i in range(KC)]
                    st = [A.view(O_R4 + i * 4224, [NQ], F32) for i in range(2)]
                    bST = [Buf("st0"), Buf("st1")]
                    sST = sLD[0:2]
                    sqx = A.view(O_R4 + 8448, [NQ], F32)
                    bSQX = Buf("sqx")
                    phase(bST + [bSQX], r4_all)
                    for kc in range(KC):
                        i = kc % 2
                        P.dma("sp", sST[i], st[i], x1v[kc], writes=[bST[i]])
                        if kc == 0:
                            P.op("dve", lambda e, i=i: e.tensor_tensor(out=acc[:, 0:NQ], in0=st[i], in1=st[i], op=ALU.mult),
                                 reads=[bST[i]], writes=[bACC])
                        else:
                            P.op("act", lambda e, i=i: e.activation(out=sqx, in_=st[i], func=AF.Square),
                                 reads=[bST[i]], writes=[bSQX])
                            P.op("dve", lambda e: e.tensor_tensor(out=acc[:, 0:NQ], in0=acc[:, 0:NQ], in1=sqx, op=ALU.add),
                                 reads=[bSQX, bACC], writes=[bACC])
                rstd_from_sumsq(NQ, QCH)
                h1T = R1
                st2 = [A.view(O_R4 + i * 4224, [NQ], F32) for i in range(2)]
                bST2 = [Buf("st2_0"), Buf("st2_1")]
                phase(bST2, r4_all)
                sST2 = sLD[0:2]
                for kc in range(KC):
                    i = kc % 2
                    P.dma("sp", sST2[i], st2[i], x1v[kc], reads=[bX1D[kc]], writes=[bST2[i]])
                    P.op("dve", lambda e, i=i, kc=kc: e.scalar_tensor_tensor(
                        out=h1T[:, kc, :], in0=st2[i], scalar=pc("g1", kc), in1=rstd[:, 0:NQ],
                        op0=ALU.mult, op1=ALU.mult), reads=[bST2[i], bRSTD, bPRM], writes=[bR1[kc]])

                cst = A.view(O_R2, [KC, OWN], BF16)
                bC = [Buf("c%d" % f) for f in range(KC)]
                sg = [A.view(O_R4 + i * 4224, [NQ], F32) for i in range(2)]
                u = [A.view(O_R4 + 8448 + i * 4352, [1088], F32) for i in range(2)]
                cacc = [A.view(O_R4 + 17152 + i * 4096, [OWN], F32) for i in range(2)]
                sqc = A.view(O_R4 + 25344, [OWN], F32)
                accS = A.view(O_R4 + 29440, [OWN], F32)
                accQ = A.view(O_R4 + 33536, [OWN], F32)
                bSG = [Buf("sg0"), Buf("sg1")]
                bU = [Buf("u0"), Buf("u1")]
                bCA = [Buf("ca0"), Buf("ca1")]
                bSQC, bAS, bAQ = Buf("sqc"), Buf("accS"), Buf("accQ")
                phase(bSG + bU + bCA + [bSQC, bAS, bAQ], r4_all)
                phase(bC, r2_all)
                hch = [(lambda kc, c0=c0, n=n: h1T[:, kc, c0:c0 + n], n) for (c0, n) in QCH]
                dwo = POFF["dww"][0]
                for f in range(KC):
                    i = f % 2
                    wi = load_w([(slice(0, 128), w_in1[:, :, 4096 + 128 * f: 4096 + 128 * f + 128])])
                    formA(wi, hch, [0, 1, 2], r1b)
                    for c, (t0, n) in enumerate(QCH):
                        P.op("act", lambda e, c=c, t0=t0, n=n, i=i, f=f: e.activation(
                            out=sg[i][:, t0:t0 + n], in_=banks[c][:, 0:n], func=AF.Sigmoid, bias=pc("bg", f)),
                            reads=[bankB[c], bPRM], writes=[bSG[i]])
                    wi = load_w([(slice(0, 128), w_in1[:, :, 128 * f: 128 * f + 128])])
                    formA(wi, hch, [3, 4, 5], r1b)
                    for c, (t0, n) in enumerate(QCH):
                        P.op("dve", lambda e, c=c, t0=t0, n=n, i=i, f=f: e.scalar_tensor_tensor(
                            out=u[i][:, t0:t0 + n], in0=banks[3 + c][:, 0:n], scalar=pc("ba", f),
                            in1=sg[i][:, t0:t0 + n], op0=ALU.add, op1=ALU.mult),
                            reads=[bankB[3 + c], bSG[i], bPRM], writes=[bU[i]])
                    P.op("dve", lambda e, i=i: e.tensor_scalar(
                        out=u[i][:, 0:HQ], in0=u[i][:, 0:HQ], scalar1=pc("flag"), scalar2=None, op0=ALU.mult),
                        reads=[bU[i], bPRM], writes=[bU[i]])
                    ca = cacc[i]
                    P.op("dve", lambda e, i=i, f=f, ca=ca: e.tensor_scalar(
                        out=ca, in0=u[i][:, 2:2 + OWN], scalar1=prm[:, dwo + CW * f: dwo + CW * f + 1],
                        scalar2=pc("dwb", f), op0=ALU.mult, op1=ALU.add),
                        reads=[bU[i], bPRM], writes=[bCA[i]])
                    for j in range(1, CW):
                        P.op("dve", lambda e, i=i, f=f, ca=ca, j=j: e.scalar_tensor_tensor(
                            out=ca, in0=u[i][:, 2 + j:2 + j + OWN], scalar=prm[:, dwo + CW * f + j: dwo + CW * f + j + 1],
                            in1=ca, op0=ALU.mult, op1=ALU.add), reads=[bU[i], bCA[i], bPRM], writes=[bCA[i]])
                    P.op("act", lambda e, f=f, ca=ca: e.activation(out=cst[:, f, :], in_=ca, func=AF.Identity),
                         reads=[bCA[i]], writes=[bC[f]])
                    if f == 0:
                        P.op("dve", lambda e, ca=ca: e.tensor_copy(out=accS, in_=ca), reads=[bCA[i]], writes=[bAS])
                        P.op("dve", lambda e, ca=ca: e.tensor_tensor(out=accQ, in0=ca, in1=ca, op=ALU.mult),
                             reads=[bCA[i]], writes=[bAQ])
                    else:
                        P.op("act", lambda e, ca=ca: e.activation(out=sqc, in_=ca, func=AF.Square),
                             reads=[bCA[i]], writes=[bSQC])
                        P.op("dve", lambda e, ca=ca: e.tensor_tensor(out=accS, in0=accS, in1=ca, op=ALU.add),
                             reads=[bCA[i], bAS], writes=[bAS])
                        P.op("dve", lambda e: e.tensor_tensor(out=accQ, in0=accQ, in1=sqc, op=ALU.add),
                             reads=[bSQC, bAQ], writes=[bAQ])

                mu = acc[:, 0:OWN]
                rl = rstd[:, 0:OWN]
                for ci, (c0, n) in enumerate(OCH):
                    P.pe(lambda e, ci=ci, c0=c0, n=n: e.matmul(banks[6][:, 0:n], lhsT=ones, rhs=accS[:, c0:c0 + n],
                                                               start=True, stop=True),
                         reads=[bONES, bAS], writes=[bankB[6]], sig=True)
                    P.pe(lambda e, ci=ci, c0=c0, n=n: e.matmul(banks[7][:, 0:n], lhsT=ones, rhs=accQ[:, c0:c0 + n],
                                                               start=True, stop=True),
                         reads=[bONES, bAQ], writes=[bankB[7]], sig=True)
                    P.op("act", lambda e, c0=c0, n=n: e.activation(out=mu[:, c0:c0 + n], in_=banks[6][:, 0:n],
                                                                    func=AF.Identity, scale=1.0 / D),
                         reads=[bankB[6]], writes=[bACC])
                    P.op("dve", lambda e, c0=c0, n=n: e.tensor_tensor(out=rl[:, c0:c0 + n], in0=mu[:, c0:c0 + n],
                                                                       in1=mu[:, c0:c0 + n], op=ALU.mult),
                         reads=[bACC], writes=[bRSTD])
                    P.op("dve", lambda e, c0=c0, n=n: e.scalar_tensor_tensor(
                        out=rl[:, c0:c0 + n], in0=banks[7][:, 0:n], scalar=1.0 / D, in1=rl[:, c0:c0 + n],
                        op0=ALU.mult, op1=ALU.subtract), reads=[bankB[7], bRSTD], writes=[bRSTD])
                    P.op("act", lambda e, c0=c0, n=n: e.activation(out=rl[:, c0:c0 + n], in_=rl[:, c0:c0 + n],
                                                                    func=AF.Sqrt, bias=pc_eps, scale=1.0),
                         reads=[bRSTD, bEPS], writes=[bRSTD])
                    P.op("dve", lambda e, c0=c0, n=n: e.reciprocal(out=rl[:, c0:c0 + n], in_=rl[:, c0:c0 + n]),
                         reads=[bRSTD], writes=[bRSTD])

                szz = [A.view(O_R4 + i * 4096, [OWN], F32) for i in range(2)]
                tt = [A.view(O_R4 + 8192 + i * 4096, [OWN], F32) for i in range(2)]
                bSZZ = [Buf("szz0"), Buf("szz1")]
                bTT = [Buf("tt0"), Buf("tt1")]
                phase(bSZZ + bTT, r4_all)
                zch = [(lambda kc, c0=c0, n=n: h1T[:, kc, HQ + c0:HQ + c0 + n], n) for (c0, n) in OCH]
                for f in range(KC):
                    i = f % 2
                    zb = [0, 1] if i == 0 else [2, 3]
                    wi = load_w([(slice(0, 128), w_in1[:, :, 8192 + 128 * f: 8192 + 128 * f + 128])])
                    formA(wi, zch, zb, r1b)
                    for c, (t0, n) in enumerate(OCH):
                        P.op("act", lambda e, c=c, t0=t0, n=n, i=i, f=f, zb=zb: e.activation(
                            out=szz[i][:, t0:t0 + n], in_=banks[zb[c]][:, 0:n], func=AF.Silu, bias=pc("bz1", f)),
                            reads=[bankB[zb[c]], bPRM], writes=[bSZZ[i]])
                    P.op("dve", lambda e, i=i, f=f: e.tensor_tensor(out=tt[i], in0=cst[:, f, :], in1=mu, op=ALU.subtract),
                         reads=[bC[f], bACC], writes=[bTT[i]])
                    P.op("dve", lambda e, i=i: e.tensor_tensor(out=tt[i], in0=tt[i], in1=rl, op=ALU.mult),
                         reads=[bTT[i], bRSTD], writes=[bTT[i]])
                    P.op("act", lambda e, i=i, f=f: e.activation(out=tt[i], in_=tt[i], func=AF.Silu,
                                                                 bias=pc("lnb", f), scale=pc("lng", f)),
                         reads=[bTT[i], bPRM], writes=[bTT[i]])
                    P.op("dve", lambda e, i=i, f=f: e.tensor_tensor(out=cst[:, f, :], in0=tt[i], in1=szz[i], op=ALU.mult),
                         reads=[bTT[i], bSZZ[i]], writes=[bC[f]])

                xr1 = [A.view(O_R4 + i * 4096, [OWN], F32) for i in range(2)]
                x2s = [A.view(O_R4 + 8192 + i * 4096, [OWN], F32) for i in range(2)]
                ys = [A.view(O_R4 + 16384 + i * 4096, [OWN], F32) for i in range(2)]
                sq2 = A.view(O_R4 + 24576, [OWN], F32)
                bXR1 = [Buf("xr1_0"), Buf("xr1_1")]
                bX2 = [Buf("x2_0"), Buf("x2_1")]
                bYS = [Buf("ys0"), Buf("ys1")]
                bSQ2 = Buf("sq2")
                phase(bXR1 + bX2 + bYS + [bSQ2], r4_all)
                sXR1 = sLD[0:2]
                sYS = sSTO
                bYD = [Buf("yd%d" % i) for i in range(KC)]
                yv = y_d.ap()
                cch = [(lambda kc, c0=c0, n=n: cst[:, kc, c0:c0 + n], n) for (c0, n) in OCH]
                a2 = acc[:, 0:OWN]
                for ob in range(KC):
                    i = ob % 2
                    zb = [0, 1] if i == 0 else [2, 3]
                    P.dma("sp", sXR1[i], xr1[i], x1v[ob][:, HQ:NQ], reads=[bX1D[ob]], writes=[bXR1[i]])
                    wi = load_w([(slice(0, 128), w_out1[:, :, 128 * ob:128 * ob + 128])])
                    formA(wi, cch, zb, lambda kc: [bC[kc]])
                    for c, (t0, n) in enumerate(OCH):
                        P.op("dve", lambda e, c=c, t0=t0, n=n, i=i, ob=ob, zb=zb: e.scalar_tensor_tensor(
                            out=x2s[i][:, t0:t0 + n], in0=banks[zb[c]][:, 0:n], scalar=pc("bo1", ob),
                            in1=xr1[i][:, t0:t0 + n], op0=ALU.add, op1=ALU.add),
                            reads=[bankB[zb[c]], bXR1[i], bPRM], writes=[bX2[i]])
                    if ob == 0:
                        P.op("dve", lambda e, i=i: e.tensor_tensor(out=a2, in0=x2s[i], in1=x2s[i], op=ALU.mult),
                             reads=[bX2[i]], writes=[bACC])
                    else:
                        P.op("act", lambda e, i=i: e.activation(out=sq2, in_=x2s[i], func=AF.Square),
                             reads=[bX2[i]], writes=[bSQ2])
                        P.op("dve", lambda e: e.tensor_tensor(out=a2, in0=a2, in1=sq2, op=ALU.add),
                             reads=[bSQ2, bACC], writes=[bACC])
                    P.op("act", lambda e, i=i, ob=ob: e.activation(out=ys[i], in_=x2s[i], func=AF.Identity,
                                                                   scale=pc("fg", ob)),
                         reads=[bX2[i], bPRM], writes=[bYS[i]])
                    P.dma("sp", sYS[i], yv[ob], ys[i], reads=[bYS[i]], writes=[bYD[ob]])
                rstd_from_sumsq(OWN, OCH)
                yi = [A.view(O_R4 + i * 4096, [OWN], F32) for i in range(2)]
                yo = [A.view(O_R4 + 8192 + i * 4096, [OWN], F32) for i in range(2)]
                bYI = [Buf("yi0"), Buf("yi1")]
                bYO = [Buf("yo0"), Buf("yo1")]
                phase(bYI + bYO, r4_all)
                sYI = sLD[0:2]
                sYO = sSTO
                for ob in range(KC):
                    i = ob % 2
                    P.dma("sp", sYI[i], yi[i], yv[ob], reads=[bYD[ob]], writes=[bYI[i]])
                    P.op("dve", lambda e, i=i: e.tensor_tensor(out=yo[i], in0=yi[i], in1=rstd[:, 0:OWN], op=ALU.mult),
                         reads=[bYI[i], bRSTD], writes=[bYO[i]])
                    P.dma("sp", sYO[i], yv[ob], yo[i], reads=[bYO[i], bYD[ob]], writes=[bYD[ob]])
                P.wait_all("sp", bYD)

        for ps in range(npass):
            emit_pass(ps)

        with nc.Block() as block:
            @block.sync
            def _(e):
                for f in P.q["sp"]:
                    f(e)

            @block.gpsimd
            def _(e):
                for f in P.q["pool"]:
                    f(e)

            @block.tensor
            def _(e):
                for f in P.q["pe"]:
                    f(e)

            @block.scalar
            def _(e):
                for f in P.q["act"]:
                    f(e)

            @block.vector
            def _(e):
                for f in P.q["dve"]:
                    f(e)
    return nc


def _pack_params(inp, j):
    prm = np.zeros((128, NPAR), np.float32)

    def put(name, arr):
        o0, n = POFF[name]
        assert arr.shape == (128, n), (name, arr.shape, n)
        prm[:, o0:o0 + n] = arr
    put("g0", _pcol(inp["norm_g"][0]))
    put("g1", _pcol(inp["norm_g"][1]))
    put("fg", _pcol(inp["final_g"]))
    b = np.asarray(inp["attn_b_in"][0], np.float32)
    put("bq", _pcol(b[0:4096]))
    bk = b[4096:4608].reshape(8, 64).T
    put("bk", np.concatenate([bk, bk], axis=0))
    put("bz", _pcol(b[5120:9216]))
    put("bv", np.broadcast_to(b[4608:5120], (128, 512)))
    put("bo0", _pcol(inp["attn_b_out"][0]))
    b1 = np.asarray(inp["conv_b_in"][0], np.float32)
    put("ba", _pcol(b1[0:4096]))
    put("bg", _pcol(b1[4096:8192]))
    put("bz1", _pcol(b1[8192:12288]))
    dw = np.asarray(inp["conv_dw_w"][0], np.float32)
    put("dww", np.ascontiguousarray(dw.T.reshape(32, 128, CW).transpose(1, 0, 2)).reshape(128, 32 * CW))
    put("dwb", _pcol(inp["conv_dw_b"][0]))
    put("lng", _pcol(inp["conv_ln_g"][0]))
    put("lnb", _pcol(inp["conv_ln_b"][0]))
    put("bo1", _pcol(inp["conv_b_out"][0]))
    s = np.asarray(inp["attn_sinks"][0], np.float32).reshape(8, 4, 2)
    sp = np.concatenate([s[:, :, 0], s[:, :, 1]], axis=1).reshape(64)
    put("sinks", np.broadcast_to(sp, (128, 64)))
    put("flag", np.full((128, 1), 1.0 if j > 0 else 0.0, np.float32))
    return prm


def _masks(j):
    s = np.arange(128)[:, None]
    q = np.arange(128)[None, :]
    cur = (s <= q).astype(np.float32)
    prev = (s > q).astype(np.float32)
    first = prev * (1.0 if j > 0 else 0.0)
    hp = np.zeros((128, 128), np.float32)
    hp[0:32, 0:32] = (s[0:32] > q[:, 0:32]).astype(np.float32)
    return np.ascontiguousarray(np.stack([cur, prev, first, hp], axis=1)).astype(ml_dtypes.bfloat16)


def _x_shard(x, c):
    b, j = c // 4, c % 4
    s = j * OWN
    xs = np.zeros((NK, D), np.float32)
    lo = s - HK
    if lo < 0:
        xs[-lo:] = x[b, 0:s + OWN]
    else:
        xs[:] = x[b, lo:s + OWN]
    return np.ascontiguousarray(xs.T).reshape(KC, 128, NK)


_NC_CACHE = {}

NCU = 4
NPASS = NCORES // NCU


def _get_nc(mode, npass):
    key = (mode, npass)
    if key not in _NC_CACHE:
        _NC_CACHE[key] = build(mode, npass)
    return _NC_CACHE[key]


def kernel(**inputs):
    inp = {k: np.asarray(v) for k, v in inputs.items()}
    x = np.asarray(inp["x"], np.float32)
    w_in0 = np.ascontiguousarray(inp["attn_w_in"][0], dtype=np.float32)
    w_out0 = np.ascontiguousarray(inp["attn_w_out"][0], dtype=np.float32)
    w_in1 = np.ascontiguousarray(inp["conv_w_in"][0], dtype=np.float32)
    w_out1 = np.ascontiguousarray(inp["conv_w_out"][0], dtype=np.float32)
    prm = [_pack_params(inp, j) for j in range(2)]
    msk = [_masks(j) for j in range(2)]
    in_maps = []
    for c in range(NCU):
        m = {"attn_w_in": w_in0, "attn_w_out": w_out0, "conv_w_in": w_in1, "conv_w_out": w_out1}
        for ps in range(NPASS):
            v = c * NPASS + ps
            sfx = "" if NPASS == 1 else "_%d" % ps
            jj = min(v % 4, 1)
            m["params" + sfx] = prm[jj]
            m["masks" + sfx] = msk[jj]
            m["xT" + sfx] = _x_shard(x, v)
        in_maps.append(m)
    res = run_bass_kernel_spmd(_get_nc("fused", NPASS), in_maps, core_ids=list(range(NCU)))
    out = np.empty((2, 4096, D), np.float32)
    for c in range(NCU):
        for ps in range(NPASS):
            v = c * NPASS + ps
            b, j = v // 4, v % 4
            sfx = "" if NPASS == 1 else "_%d" % ps
            yT = np.asarray(res.results[c]["yT" + sfx], np.float32).reshape(D, OWN)
            out[b, j * OWN:(j + 1) * OWN, :] = yT.T
    return out
```

```python
import contextlib
import numpy as np
import ml_dtypes
import concourse.bass as bass
import concourse.mybir as mybir
from concourse.bass_utils import run_bass_kernel_spmd

F32 = mybir.dt.float32
BF16 = mybir.dt.bfloat16
ALU = mybir.AluOpType
AF = mybir.ActivationFunctionType

D = 4096
KC = 32
OWN = 1024
HQ = 32
NQ = OWN + HQ
HK = 160
NK = OWN + HK
NCORES = 8
EPS = 1e-5
CW = 31

QCH = [(0, 352), (352, 352), (704, 352)]
OCH = [(0, 512), (512, 512)]


def _param_layout():
    off = {}
    cur = 0
    for name, n in [("g0", 32), ("g1", 32), ("fg", 32), ("bq", 32), ("bk", 8), ("bz", 32),
                    ("bo0", 32), ("ba", 32), ("bg", 32), ("bz1", 32), ("dww", 32 * CW),
                    ("dwb", 32), ("lng", 32), ("lnb", 32), ("bo1", 32), ("sinks", 64),
                    ("flag", 1), ("bv", 512)]:
        off[name] = (cur, n)
        cur += n
    return off, cur


POFF, NPAR = _param_layout()


def _pcol(v):
    return np.ascontiguousarray(np.asarray(v, np.float32).reshape(-1, 128).T)


class Sem:
    def __init__(self, h):
        self.h = h
        self.n = 0


class Buf:
    def __init__(self, name):
        self.name = name
        self.w = {}
        self.r = {}
        self.alias = []


class Prog:
    def __init__(self, nc, stack):
        self.nc = nc
        self.stack = stack
        self.q = {k: [] for k in ("pe", "act", "dve", "pool", "sp")}
        self.waited = {k: {} for k in self.q}
        self.nsem = 0
        self.esem = {k: self.sem("e_" + k) for k in ("pe", "act", "dve", "pool")}
        self.pend_r = set()
        self.pend_w = set()

    def sem(self, name):
        self.nsem += 1
        return Sem(self.stack.enter_context(self.nc.semaphore("%s_%d" % (name, self.nsem))))

    def _deps(self, eng, reads, writes):
        need = {}
        for b in reads:
            for s, v in b.w.items():
                need[s] = max(need.get(s, 0), v)
        for b in writes:
            for bb in [b] + b.alias:
                for s, v in bb.w.items():
                    need[s] = max(need.get(s, 0), v)
                for s, v in bb.r.items():
                    need[s] = max(need.get(s, 0), v)
        out = []
        wd = self.waited[eng]
        for s, v in need.items():
            if wd.get(s, 0) < v:
                wd[s] = v
                out.append((s.h, v))
        return out

    @staticmethod
    def _mark(ev, reads, writes):
        s, v = ev
        for b in writes:
            b.w = {s: v}
            b.r = {}
        for b in reads:
            if b in writes:
                continue
            b.r[s] = max(b.r.get(s, 0), v)

    def op(self, eng, fn, reads=(), writes=()):
        waits = self._deps(eng, reads, writes)
        s = self.esem[eng]
        s.n += 1
        ev = (s, s.n)

        def run(e, waits=waits, fn=fn, h=s.h):
            for hh, v in waits:
                e.wait_ge(hh, v)
            fn(e).then_inc(h, 1)
        self.q[eng].append(run)
        self._mark(ev, reads, writes)
        return ev

    def pe(self, fn, reads=(), writes=(), sig=False):
        waits = self._deps("pe", reads, writes)
        self.pend_r |= set(reads)
        self.pend_w |= set(writes)
        s = self.esem["pe"]
        if sig:
            s.n += 1
            ev = (s, s.n)
            self._mark(ev, [b for b in self.pend_r if b not in self.pend_w], list(self.pend_w))
            self.pend_r = set()
            self.pend_w = set()

        def run(e, waits=waits, fn=fn, h=s.h, sig=sig):
            for hh, v in waits:
                e.wait_ge(hh, v)
            ins = fn(e)
            if sig:
                ins.then_inc(h, 1)
        self.q["pe"].append(run)

    def dma(self, eng, sem, out, in_, reads=(), writes=(), n=1):
        waits = self._deps(eng, reads, writes)
        sem.n += 16
        ev = (sem, sem.n)

        def run(e, waits=waits, out=out, in_=in_, h=sem.h):
            for hh, v in waits:
                e.wait_ge(hh, v)
            e.dma_start(out=out, in_=in_).then_inc(h, 16)
        self.q[eng].append(run)
        self._mark(ev, reads, writes)
        return ev

    def wait_all(self, eng, bufs):
        waits = self._deps(eng, bufs, bufs)

        def run(e, waits=waits):
            for hh, v in waits:
                e.wait_ge(hh, v)
        self.q[eng].append(run)


class Arena:
    def __init__(self, t, nbytes):
        self.t = t
        self.nbytes = nbytes

    def view(self, off, shape, dt):
        assert off % 32 == 0
        esz = 4 if dt == F32 else 2
        n = int(np.prod(shape))
        assert off + n * esz <= self.nbytes, (off, n * esz, self.nbytes)
        v = self.t[:, off // 4: off // 4 + (n * esz + 3) // 4]
        if dt != F32:
            v = v.bitcast(dt)[:, 0:n]
        if len(shape) == 2:
            v = v.rearrange("p (a b) -> p a b", a=shape[0])
        elif len(shape) == 3:
            v = v.rearrange("p (a b c) -> p a b c", a=shape[0], b=shape[1])
        return v


def _al(x):
    return (x + 31) // 32 * 32


def build(mode="fused", npass=1):
    do_l0 = mode in ("fused", "l0")
    do_l1 = mode in ("fused", "l1")
    nc = bass.Bass("TRN2", target_bir_lowering=False)

    def din(name, shape, dt=F32):
        return nc.dram_tensor(name, list(shape), dt, kind="ExternalInput")

    if do_l0:
        w_in0 = din("attn_w_in", [D, 9216]).ap().rearrange("(kc p) n -> p kc n", p=128)
        w_out0 = din("attn_w_out", [D, D]).ap().rearrange("(kc p) n -> p kc n", p=128)
    if do_l1:
        w_in1 = din("conv_w_in", [D, 12288]).ap().rearrange("(kc p) n -> p kc n", p=128)
        w_out1 = din("conv_w_out", [D, D]).ap().rearrange("(kc p) n -> p kc n", p=128)
    x1kind = {"fused": "Internal", "l0": "ExternalOutput", "l1": "ExternalInput"}[mode]

    o = 0
    O_PRM = o; o = _al(o + NPAR * 4)
    O_ONES = o; o = _al(o + 128 * 4)
    O_MSK = o; o = _al(o + 4 * 128 * 2)
    O_SINK = o; o = _al(o + 64 * 4)
    O_RSTD = o; o = _al(o + NK * 4)
    O_ACC = o; o = _al(o + NK * 4)
    O_R1 = o; o = _al(o + KC * NQ * 2)
    O_R2 = o; o = _al(o + KC * OWN * 2)
    NW = 2
    O_W = o; o = _al(o + NW * KC * 128 * 2)
    O_R4 = o; o = _al(o + 37632)
    TOTAL = o
    assert TOTAL <= 212000, TOTAL

    with contextlib.ExitStack() as stack:
        arena_t = stack.enter_context(nc.sbuf_tensor("arena", [128, TOTAL // 4], F32))
        banks = [stack.enter_context(nc.psum_tensor("bank%d" % i, [128, 512], F32)) for i in range(8)]
        A = Arena(arena_t, TOTAL)
        P = Prog(nc, stack)
        bankB = [Buf("bank%d" % i) for i in range(8)]

        prm = A.view(O_PRM, [NPAR], F32)
        ones = A.view(O_ONES, [128], F32)
        msk = A.view(O_MSK, [4, 128], BF16)
        sinkexp = A.view(O_SINK, [64], F32)
        rstd = A.view(O_RSTD, [NK], F32)
        acc = A.view(O_ACC, [NK], F32)
        R1 = A.view(O_R1, [KC, NQ], BF16)
        wslot = [A.view(O_W + i * KC * 128 * 2, [KC, 128], BF16) for i in range(NW)]

        def pc(name, j=0, n=1, rows=slice(0, 128)):
            o0, _ = POFF[name]
            return prm[rows, o0 + j: o0 + j + n]

        bPRM, bONES, bMSK, bSINK, bRSTD, bACC = (Buf(n) for n in
                                                  ("prm", "ones", "msk", "sink", "rstd", "acc"))
        bR1 = [Buf("R1_%d" % k) for k in range(KC)]
        r1b = lambda kc: [bR1[kc]]
        bW = [Buf("w%d" % i) for i in range(NW)]
        r4_all = []
        r2_all = []

        def phase(bufs, pool):
            for b in bufs:
                b.alias = list(pool)
            pool.extend(bufs)
        sW = [P.sem("wld%d" % i) for i in range(NW)]
        sLD = [P.sem("ld%d" % i) for i in range(3)]
        sSTO = [P.sem("sto%d" % i) for i in range(2)]
        sPRM = [P.sem("prm"), P.sem("msk")]
        sWv = P.sem("wv")
        sGL = P.sem("gld")
        wctr = [0]

        P.op("dve", lambda e: e.memset(ones, 1.0), writes=[bONES])

        def load_w(src_ap_list):
            i = wctr[0] % NW
            wctr[0] += 1
            for (dst_sl, src) in src_ap_list:
                P.dma("pool", sW[i], wslot[i][:, :, dst_sl], src, writes=[bW[i]])
            return i

        def formA(wi, act_chunks, out_banks, act_bufs):
            nch = len(act_chunks)
            for kc in range(KC):
                for c in range(nch):
                    apf, n = act_chunks[c]
                    bk = out_banks[c]
                    P.pe(lambda e, bk=bk, n=n, kc=kc, apf=apf: e.matmul(
                        banks[bk][:, 0:n], lhsT=wslot[wi][:, kc, :], rhs=apf(kc),
                        start=(kc == 0), stop=(kc == KC - 1)),
                        reads=[bW[wi]] + act_bufs(kc), writes=[bankB[bk]], sig=(kc == KC - 1))

        def bcast_stat(src_ap, n_tot, chunks, fn_evac):
            for ci, (c0, n) in enumerate(chunks):
                P.pe(lambda e, ci=ci, c0=c0, n=n: e.matmul(
                    banks[ci][:, 0:n], lhsT=ones, rhs=src_ap[:, c0:c0 + n], start=True, stop=True),
                    reads=[bONES, bACC], writes=[bankB[ci]], sig=True)
                fn_evac(ci, c0, n)

        def rstd_from_sumsq(n_tot, chunks):
            def ev(ci, c0, n):
                P.op("act", lambda e: e.activation(out=rstd[:, c0:c0 + n], in_=banks[ci][:, 0:n],
                                                   func=AF.Sqrt, bias=pc_eps, scale=1.0 / D),
                     reads=[bankB[ci], bEPS], writes=[bRSTD])
                P.op("dve", lambda e: e.reciprocal(out=rstd[:, c0:c0 + n], in_=rstd[:, c0:c0 + n]),
                     reads=[bRSTD], writes=[bRSTD])
            bcast_stat(acc, n_tot, chunks, ev)

        eps_t = stack.enter_context(nc.sbuf_tensor("epsc", [128, 1], F32))
        pc_eps = eps_t[:, 0:1]
        bEPS = Buf("eps")
        P.op("dve", lambda e: e.memset(eps_t[:], EPS), writes=[bEPS])

        def emit_pass(ps):
            sfx = "" if npass == 1 else "_%d" % ps
            prm_d = din("params" + sfx, [128, NPAR]).ap()
            msk_d = din("masks" + sfx, [128, 4, 128], BF16).ap()
            P.dma("sp", sPRM[0], prm, prm_d, writes=[bPRM])
            P.dma("sp", sPRM[1], msk, msk_d, writes=[bMSK])
            if do_l0:
                xT_d = din("xT" + sfx, [KC, 128, NK])
                g_d = nc.dram_tensor("gsc" + sfx, [KC, 128, NQ], BF16, kind="Internal")
            if do_l1:
                y_d = nc.dram_tensor("yT" + sfx, [KC, 128, OWN], F32, kind="ExternalOutput")
            x1_d = nc.dram_tensor("x1sc" + sfx, [KC, 128, NQ], F32, kind=x1kind)
            if do_l0:
                hT = R1
                KT = A.view(O_R2, [8, NK], BF16)
                Vaug = A.view(O_R2 + 18944, [10, 1088], BF16)
                QT = [A.view(O_R2 + 18944 + 21760 + i * 8448, [4, NQ], BF16) for i in range(2)]
                Wv = A.view(O_R2 + 18944 + 21760, [KC, 256], BF16)
                bKT, bV, bWv = Buf("KT"), Buf("Vaug"), Buf("Wv")
                bQT = [Buf("QT0"), Buf("QT1")]
                xs = [A.view(O_R4 + i * 4736, [NK], F32) for i in range(3)]
                bXS = [Buf("xs%d" % i) for i in range(3)]
                sXS = sLD
                hTe = A.view(O_R4 + 14208, [KC, HK], BF16)
                bHTe = Buf("hTe")
                sq_t = A.view(O_R4 + 14208 + 10240, [NK], F32)
                bSQ = Buf("sq")
                phase(bXS + [bHTe, bSQ], r4_all)
                phase([bKT, bV, bWv], r2_all)
                phase(bQT, r2_all)
                sz = [A.view(O_R4 + i * 8448, [4, NQ], BF16) for i in range(2)]
                bSZ = [Buf("sz0"), Buf("sz1")]
                PT = [A.view(O_R4 + 16896 + i * 4096, [4, 512], BF16) for i in range(2)]
                bPT = [[Buf("pt%d_%d" % (i, s)) for s in range(4)] for i in range(2)]
                rb = [A.view(O_R4 + 25088 + i * 2048, [512], F32) for i in range(2)]
                bRB = [Buf("rb0"), Buf("rb1")]
                G = A.view(O_R4 + 29184, [4, NQ], BF16)
                bG = Buf("G")
                phase(bSZ + bPT[0] + bPT[1] + bRB + [bG], r4_all)
                sG = sSTO[0]

                P.op("act", lambda e: e.activation(out=sinkexp, in_=pc("sinks", 0, 64), func=AF.Exp),
                     reads=[bPRM], writes=[bSINK])

                xTv = xT_d.ap()
                for kc in range(KC):
                    i = kc % 3
                    P.dma("sp", sXS[i], xs[i], xTv[kc], writes=[bXS[i]])
                    if kc == 0:
                        P.op("dve", lambda e, i=i: e.tensor_tensor(out=acc, in0=xs[i], in1=xs[i], op=ALU.mult),
                             reads=[bXS[i]], writes=[bACC])
                    else:
                        P.op("act", lambda e, i=i: e.activation(out=sq_t, in_=xs[i], func=AF.Square),
                             reads=[bXS[i]], writes=[bSQ])
                        P.op("dve", lambda e: e.tensor_tensor(out=acc, in0=acc, in1=sq_t, op=ALU.add),
                             reads=[bSQ, bACC], writes=[bACC])
                rstd_from_sumsq(NK, [(0, 512), (512, 512), (1024, 160)])
                for kc in range(KC):
                    i = kc % 3
                    P.dma("sp", sXS[i], xs[i], xTv[kc], writes=[bXS[i]])
                    P.op("dve", lambda e, i=i, kc=kc: e.scalar_tensor_tensor(
                        out=hT[:, kc, :], in0=xs[i][:, 128:NK], scalar=pc("g0", kc), in1=rstd[:, 128:NK],
                        op0=ALU.mult, op1=ALU.mult), reads=[bXS[i], bRSTD, bPRM], writes=[bR1[kc]])
                    P.op("dve", lambda e, i=i, kc=kc: e.scalar_tensor_tensor(
                        out=hTe[:, kc, :], in0=xs[i][:, 0:HK], scalar=pc("g0", kc), in1=rstd[:, 0:HK],
                        op0=ALU.mult, op1=ALU.mult), reads=[bXS[i], bRSTD, bPRM], writes=[bHTe])

                kch = [(lambda kc: hTe[:, kc, 0:HK], HK),
                       (lambda kc: hT[:, kc, 32:544], 512),
                       (lambda kc: hT[:, kc, 544:1056], 512)]
                kdst = [(0, HK), (HK, 512), (HK + 512, 512)]
                for h in range(8):
                    c0 = 4096 + 64 * h
                    wi = load_w([(slice(0, 64), w_in0[:, :, c0:c0 + 64]), (slice(64, 128), w_in0[:, :, c0:c0 + 64])])
                    formA(wi, kch, [0, 1, 2], lambda kc: [bR1[kc], bHTe])
                    for c in range(3):
                        d0, n = kdst[c]
                        P.op("act", lambda e, c=c, d0=d0, n=n, h=h: e.activation(
                            out=KT[:, h, d0:d0 + n], in_=banks[c][:, 0:n], func=AF.Identity, bias=pc("bk", h)),
                            reads=[bankB[c], bPRM], writes=[bKT])

                P.op("dve", lambda e: e.memset(Vaug, 1.0), writes=[bV])
                vt = [(lambda kc: hTe[:, kc, 0:32], 32), (lambda kc: hTe[:, kc, 32:160], 128)]
                for n_ in range(8):
                    vt.append((lambda kc, n_=n_: hT[:, kc, 32 + 128 * n_: 160 + 128 * n_], 128))
                bvo = POFF["bv"][0]
                for hf in range(2):
                    c0 = 4608 + 256 * hf
                    P.dma("pool", sWv, Wv, w_in0[:, :, c0:c0 + 256], writes=[bWv])
                    for t in range(10):
                        apf, m = vt[t]
                        bk = 3 + (t % 2)
                        for kc in range(KC):
                            P.pe(lambda e, bk=bk, m=m, kc=kc, apf=apf: e.matmul(
                                banks[bk][0:m, 0:256], lhsT=apf(kc), rhs=Wv[:, kc, :],
                                start=(kc == 0), stop=(kc == KC - 1)),
                                reads=[bWv, bR1[kc], bHTe], writes=[bankB[bk]], sig=(kc == KC - 1))
                        P.op("dve", lambda e, bk=bk, m=m, t=t, hf=hf: e.tensor_tensor(
                            out=Vaug[0:m, t, 64 + 512 * hf: 64 + 512 * hf + 512].rearrange("p (k c) -> p k c", k=4)[:, :, 0:64],
                            in0=banks[bk][0:m, 0:256].rearrange("p (k c) -> p k c", k=4),
                            in1=prm[0:m, bvo + 256 * hf: bvo + 256 * hf + 256].rearrange("p (k c) -> p k c", k=4),
                            op=ALU.add), reads=[bankB[bk], bPRM], writes=[bV])

                qch = [(lambda kc, c0=c0, n=n: hT[:, kc, c0:c0 + n], n) for (c0, n) in QCH]
                sctr = [0]

                def attn_S(g, n):
                    qb = g % 2
                    pb = (n + 1) % 2
                    if n < 0:
                        qs, nq = 0, 32
                        keys = [(0, 32, msk[0:32, 3, 0:32]), (32, 128, msk[:, 0, 96:128])]
                    else:
                        qs, nq = 32 + 128 * n, 128
                        keys = [(32 + 128 * n, 128, msk[:, 2, :] if n == 0 else msk[:, 1, :]),
                                (160 + 128 * n, 128, msk[:, 0, :])]
                    for half in range(2):
                        rows = slice(64 * half, 64 * half + 64)
                        for kb in range(2):
                            k0, nk, m_ap = keys[kb]
                            slot = 2 * half + kb
                            sb = 3 + (sctr[0] % 2)
                            sctr[0] += 1
                            P.pe(lambda e, sb=sb, nk=nk, nq=nq, rows=rows, k0=k0, qs=qs, qb=qb, g=g: e.matmul(
                                banks[sb][0:nk, 0:4 * nq], lhsT=KT[rows, g, k0:k0 + nk],
                                rhs=QT[qb][rows, :, qs:qs + nq], start=True, stop=True),
                                reads=[bKT, bQT[qb]], writes=[bankB[sb]], sig=True)
                            P.op("act", lambda e, sb=sb, nk=nk, nq=nq, pb=pb, slot=slot: e.activation(
                                out=PT[pb][0:nk, slot, 0:4 * nq], in_=banks[sb][0:nk, 0:4 * nq],
                                func=AF.Exp, scale=0.125), reads=[bankB[sb]], writes=[bPT[pb][slot]])
                            P.op("dve", lambda e, nk=nk, nq=nq, pb=pb, slot=slot, m_ap=m_ap: e.tensor_tensor(
                                out=PT[pb][0:nk, slot, 0:4 * nq].rearrange("p (h q) -> p h q", h=4),
                                in0=PT[pb][0:nk, slot, 0:4 * nq].rearrange("p (h q) -> p h q", h=4),
                                in1=m_ap[:, None, :].to_broadcast([nk, 4, nq]), op=ALU.mult),
                                reads=[bPT[pb][slot], bMSK], writes=[bPT[pb][slot]])

                def attn_PV(g, n):
                    qb = g % 2
                    pb = (n + 1) % 2
                    if n < 0:
                        qs, nq = 0, 32
                        tiles = [(0, 32), (1, 128)]
                    else:
                        qs, nq = 32 + 128 * n, 128
                        tiles = [(1 + n, 128), (2 + n, 128)]
                    for half in range(2):
                        ob = 5 + half
                        vc0 = 64 + 128 * g if half == 0 else 128 * g
                        for kb in range(2):
                            t, nk = tiles[kb]
                            slot = 2 * half + kb
                            P.pe(lambda e, ob=ob, nq=nq, nk=nk, t=t, vc0=vc0, pb=pb, slot=slot, kb=kb: e.matmul(
                                banks[ob][:, 0:4 * nq], lhsT=Vaug[0:nk, t, vc0:vc0 + 128],
                                rhs=PT[pb][0:nk, slot, 0:4 * nq], start=(kb == 0), stop=(kb == 1)),
                                reads=[bV, bPT[pb][slot]], writes=[bankB[ob]], sig=(kb == 1))
                        num = slice(0, 64) if half == 0 else slice(64, 128)
                        den = slice(64, 128) if half == 0 else slice(0, 64)
                        so = POFF["sinks"][0]
                        r = rb[half]
                        w4 = 4 * nq
                        P.op("dve", lambda e, ob=ob, den=den, num=num, g=g, half=half, r=r, nq=nq, w4=w4: e.tensor_tensor(
                            out=r[num, 0:w4].rearrange("p (h q) -> p h q", h=4),
                            in0=banks[ob][den, 0:w4].rearrange("p (h q) -> p h q", h=4),
                            in1=sinkexp[den, 8 * g + 4 * half: 8 * g + 4 * half + 4][:, :, None].to_broadcast([64, 4, nq]),
                            op=ALU.add), reads=[bankB[ob], bSINK], writes=[bRB[half]])
                        P.op("dve", lambda e, r=r, num=num, w4=w4: e.reciprocal(out=r[num, 0:w4], in_=r[num, 0:w4]),
                             reads=[bRB[half]], writes=[bRB[half]])
                        P.op("dve", lambda e, r=r, num=num, w4=w4, qb=qb, qs=qs, nq=nq: e.tensor_tensor(
                            out=r[num, 0:w4].rearrange("p (h q) -> p h q", h=4),
                            in0=r[num, 0:w4].rearrange("p (h q) -> p h q", h=4),
                            in1=sz[qb][num, :, qs:qs + nq], op=ALU.mult),
                            reads=[bRB[half], bSZ[qb]], writes=[bRB[half]])
                        P.op("dve", lambda e, r=r, num=num, w4=w4, ob=ob, qs=qs, nq=nq: e.tensor_tensor(
                            out=G[num, :, qs:qs + nq],
                            in0=banks[ob][num, 0:w4].rearrange("p (h q) -> p h q", h=4),
                            in1=r[num, 0:w4].rearrange("p (h q) -> p h q", h=4), op=ALU.mult),
                            reads=[bankB[ob], bRB[half]], writes=[bG])

                gv = g_d.ap()
                bGS = [Buf("gsc%d" % g) for g in range(8)]

                def attn_steps(g):
                    L = [-1] + list(range(8))
                    slots = [[lambda: attn_S(g, L[0])]]
                    for i in range(8):
                        slots.append([lambda i=i: attn_PV(g, L[i]), lambda i=i: attn_S(g, L[i + 1])])
                    slots[-1].append(lambda: attn_PV(g, L[8]))

                    def fin():
                        for p in range(4):
                            P.dma("sp", sG, gv[4 * g + p], G[:, p, :], reads=[bG], writes=[bGS[g]])
                    slots[-1].append(fin)
                    return slots

                ipset = [0, 1, 2, 7]
                ipctr = [0]

                def inproj_block(g, i):
                    qb = g % 2
                    ib = [ipset[(3 * ipctr[0] + c) % 4] for c in range(3)]
                    ipctr[0] += 1
                    if i < 4:
                        c0 = 128 * (4 * g + i)
                        wi = load_w([(slice(0, 128), w_in0[:, :, c0:c0 + 128])])
                        formA(wi, qch, ib, r1b)
                        for c, (t0, n) in enumerate(QCH):
                            P.op("act", lambda e, c=c, t0=t0, n=n, i=i, qb=qb, g=g, ib=ib: e.activation(
                                out=QT[qb][:, i, t0:t0 + n], in_=banks[ib[c]][:, 0:n], func=AF.Identity,
                                bias=pc("bq", 4 * g + i)), reads=[bankB[ib[c]], bPRM], writes=[bQT[qb]])
                    else:
                        p = i - 4
                        c0 = 5120 + 128 * (4 * g + p)
                        wi = load_w([(slice(0, 128), w_in0[:, :, c0:c0 + 128])])
                        formA(wi, qch, ib, r1b)
                        for c, (t0, n) in enumerate(QCH):
                            P.op("act", lambda e, c=c, t0=t0, n=n, p=p, qb=qb, g=g, ib=ib: e.activation(
                                out=sz[qb][:, p, t0:t0 + n], in_=banks[ib[c]][:, 0:n], func=AF.Silu,
                                bias=pc("bz", 4 * g + p)), reads=[bankB[ib[c]], bPRM], writes=[bSZ[qb]])

                for g in range(9):
                    slots = attn_steps(g - 1) if g >= 1 else None
                    if slots is not None:
                        for f in slots[0]:
                            f()
                    for i in range(8):
                        if g < 8:
                            inproj_block(g, i)
                        if slots is not None:
                            for f in slots[i + 1]:
                                f()

                gT = R1
                for kc in range(KC):
                    P.dma("sp", sGL, gT[:, kc, :], gv[kc], reads=[bGS[kc // 4]], writes=[bR1[kc]])
                for kc in range(KC):
                    bR1[kc].w = {sGL: sGL.n}

                xres = [A.view(O_R4 + i * 4224, [NQ], F32) for i in range(2)]
                x1st = [A.view(O_R4 + 8448 + i * 4224, [NQ], F32) for i in range(2)]
                sq1 = A.view(O_R4 + 16896, [NQ], F32)
                bXR = [Buf("xr0"), Buf("xr1")]
                bX1 = [Buf("x1s0"), Buf("x1s1")]
                bSQ1 = Buf("sq1")
                phase(bXR + bX1 + [bSQ1], r4_all)
                sXR = sLD[0:2]
                sX1 = sSTO
                bX1D = [Buf("x1d%d" % i) for i in range(KC)]
                gch = [(lambda kc, c0=c0, n=n: gT[:, kc, c0:c0 + n], n) for (c0, n) in QCH]
                x1v = x1_d.ap()
                for ob in range(KC):
                    i = ob % 2
                    P.dma("sp", sXR[i], xres[i], xTv[ob][:, 128:NK], writes=[bXR[i]])
                    wi = load_w([(slice(0, 128), w_out0[:, :, 128 * ob:128 * ob + 128])])
                    ib = [0, 1, 2] if ob % 2 == 0 else [3, 4, 5]
                    formA(wi, gch, ib, r1b)
                    for c, (t0, n) in enumerate(QCH):
                        P.op("dve", lambda e, c=c, t0=t0, n=n, i=i, ob=ob, ib=ib: e.scalar_tensor_tensor(
                            out=x1st[i][:, t0:t0 + n], in0=banks[ib[c]][:, 0:n], scalar=pc("bo0", ob),
                            in1=xres[i][:, t0:t0 + n], op0=ALU.add, op1=ALU.add),
                            reads=[bankB[ib[c]], bXR[i], bPRM], writes=[bX1[i]])
                    P.dma("sp", sX1[i], x1v[ob], x1st[i], reads=[bX1[i]], writes=[bX1D[ob]])
                    if ob == 0:
                        P.op("dve", lambda e, i=i: e.tensor_tensor(out=acc[:, 0:NQ], in0=x1st[i], in1=x1st[i], op=ALU.mult),
                             reads=[bX1[i]], writes=[bACC])
                    else:
                        P.op("act", lambda e, i=i: e.activation(out=sq1, in_=x1st[i], func=AF.Square),
                             reads=[bX1[i]], writes=[bSQ1])
                        P.op("dve", lambda e: e.tensor_tensor(out=acc[:, 0:NQ], in0=acc[:, 0:NQ], in1=sq1, op=ALU.add),
                             reads=[bSQ1, bACC], writes=[bACC])
                if mode == "l0":
                    P.wait_all("sp", bX1D)

            if do_l1:
                x1v = x1_d.ap()
                if not do_l0:
                    bX1D = [Buf("x1d%d" % i) for i in range(KC)]
                    st = [A.view(O_R4 + i * 4224, [NQ], F32) for i in range(2)]
                    bST = [Buf("st0"), Buf("st1")]
                    sST = sLD[0:2]
                    sqx = A.view(O_R4 + 8448, [NQ], F32)
                    bSQX = Buf("sqx")
                    phase(bST + [bSQX], r4_all)
                    for kc in range(KC):
                        i = kc % 2
                        P.dma("sp", sST[i], st[i], x1v[kc], writes=[bST[i]])
                        if kc == 0:
                            P.op("dve", lambda e, i=i: e.tensor_tensor(out=acc[:, 0:NQ], in0=st[i], in1=st[i], op=ALU.mult),
                                 reads=[bST[i]], writes=[bACC])
                        else:
                            P.op("act", lambda e, i=i: e.activation(out=sqx, in_=st[i], func=AF.Square),
                                 reads=[bST[i]], writes=[bSQX])
                            P.op("dve", lambda e: e.tensor_tensor(out=acc[:, 0:NQ], in0=acc[:, 0:NQ], in1=sqx, op=ALU.add),
                                 reads=[bSQX, bACC], writes=[bACC])
                rstd_from_sumsq(NQ, QCH)
                h1T = R1
                st2 = [A.view(O_R4 + i * 4224, [NQ], F32) for i in range(2)]
                bST2 = [Buf("st2_0"), Buf("st2_1")]
                phase(bST2, r4_all)
                sST2 = sLD[0:2]
                for kc in range(KC):
                    i = kc % 2
                    P.dma("sp", sST2[i], st2[i], x1v[kc], reads=[bX1D[kc]], writes=[bST2[i]])
                    P.op("dve", lambda e, i=i, kc=kc: e.scalar_tensor_tensor(
                        out=h1T[:, kc, :], in0=st2[i], scalar=pc("g1", kc), in1=rstd[:, 0:NQ],
                        op0=ALU.mult, op1=ALU.mult), reads=[bST2[i], bRSTD, bPRM], writes=[bR1[kc]])

                cst = A.view(O_R2, [KC, OWN], BF16)
                bC = [Buf("c%d" % f) for f in range(KC)]
                sg = [A.view(O_R4 + i * 4224, [NQ], F32) for i in range(2)]
                u = [A.view(O_R4 + 8448 + i * 4352, [1088], F32) for i in range(2)]
                cacc = [A.view(O_R4 + 17152 + i * 4096, [OWN], F32) for i in range(2)]
                sqc = A.view(O_R4 + 25344, [OWN], F32)
                accS = A.view(O_R4 + 29440, [OWN], F32)
                accQ = A.view(O_R4 + 33536, [OWN], F32)
                bSG = [Buf("sg0"), Buf("sg1")]
                bU = [Buf("u0"), Buf("u1")]
                bCA = [Buf("ca0"), Buf("ca1")]
                bSQC, bAS, bAQ = Buf("sqc"), Buf("accS"), Buf("accQ")
                phase(bSG + bU + bCA + [bSQC, bAS, bAQ], r4_all)
                phase(bC, r2_all)
                hch = [(lambda kc, c0=c0, n=n: h1T[:, kc, c0:c0 + n], n) for (c0, n) in QCH]
                dwo = POFF["dww"][0]
                for f in range(KC):
                    i = f % 2
                    wi = load_w([(slice(0, 128), w_in1[:, :, 4096 + 128 * f: 4096 + 128 * f + 128])])
                    formA(wi, hch, [0, 1, 2], r1b)
                    for c, (t0, n) in enumerate(QCH):
                        P.op("act", lambda e, c=c, t0=t0, n=n, i=i, f=f: e.activation(
                            out=sg[i][:, t0:t0 + n], in_=banks[c][:, 0:n], func=AF.Sigmoid, bias=pc("bg", f)),
                            reads=[bankB[c], bPRM], writes=[bSG[i]])
                    wi = load_w([(slice(0, 128), w_in1[:, :, 128 * f: 128 * f + 128])])
                    formA(wi, hch, [3, 4, 5], r1b)
                    for c, (t0, n) in enumerate(QCH):
                        P.op("dve", lambda e, c=c, t0=t0, n=n, i=i, f=f: e.scalar_tensor_tensor(
                            out=u[i][:, t0:t0 + n], in0=banks[3 + c][:, 0:n], scalar=pc("ba", f),
                            in1=sg[i][:, t0:t0 + n], op0=ALU.add, op1=ALU.mult),
                            reads=[bankB[3 + c], bSG[i], bPRM], writes=[bU[i]])
                    P.op("dve", lambda e, i=i: e.tensor_scalar(
                        out=u[i][:, 0:HQ], in0=u[i][:, 0:HQ], scalar1=pc("flag"), scalar2=None, op0=ALU.mult),
                        reads=[bU[i], bPRM], writes=[bU[i]])
                    ca = cacc[i]
                    P.op("act", lambda e, i=i, f=f, ca=ca: e.activation(
                        out=ca, in_=u[i][:, 2:2 + OWN], func=AF.Identity,
                        scale=prm[:, dwo + CW * f: dwo + CW * f + 1], bias=pc("dwb", f)),
                        reads=[bU[i], bPRM], writes=[bCA[i]])
                    for j in range(1, CW):
                        P.op("dve", lambda e, i=i, f=f, ca=ca, j=j: e.scalar_tensor_tensor(
                            out=ca, in0=u[i][:, 2 + j:2 + j + OWN], scalar=prm[:, dwo + CW * f + j: dwo + CW * f + j + 1],
                            in1=ca, op0=ALU.mult, op1=ALU.add), reads=[bU[i], bCA[i], bPRM], writes=[bCA[i]])
                    P.op("act", lambda e, f=f, ca=ca: e.activation(out=cst[:, f, :], in_=ca, func=AF.Identity),
                         reads=[bCA[i]], writes=[bC[f]])
                    if f == 0:
                        P.op("dve", lambda e, ca=ca: e.tensor_copy(out=accS, in_=ca), reads=[bCA[i]], writes=[bAS])
                        P.op("dve", lambda e, ca=ca: e.tensor_tensor(out=accQ, in0=ca, in1=ca, op=ALU.mult),
                             reads=[bCA[i]], writes=[bAQ])
                    else:
                        P.op("act", lambda e, ca=ca: e.activation(out=sqc, in_=ca, func=AF.Square),
                             reads=[bCA[i]], writes=[bSQC])
                        P.op("dve", lambda e, ca=ca: e.tensor_tensor(out=accS, in0=accS, in1=ca, op=ALU.add),
                             reads=[bCA[i], bAS], writes=[bAS])
                        P.op("dve", lambda e: e.tensor_tensor(out=accQ, in0=accQ, in1=sqc, op=ALU.add),
                             reads=[bSQC, bAQ], writes=[bAQ])

                mu = acc[:, 0:OWN]
                rl = rstd[:, 0:OWN]
                for ci, (c0, n) in enumerate(OCH):
                    P.pe(lambda e, ci=ci, c0=c0, n=n: e.matmul(banks[6][:, 0:n], lhsT=ones, rhs=accS[:, c0:c0 + n],
                                                               start=True, stop=True),
                         reads=[bONES, bAS], writes=[bankB[6]], sig=True)
                    P.pe(lambda e, ci=ci, c0=c0, n=n: e.matmul(banks[7][:, 0:n], lhsT=ones, rhs=accQ[:, c0:c0 + n],
                                                               start=True, stop=True),
                         reads=[bONES, bAQ], writes=[bankB[7]], sig=True)
                    P.op("act", lambda e, c0=c0, n=n: e.activation(out=mu[:, c0:c0 + n], in_=banks[6][:, 0:n],
                                                                    func=AF.Identity, scale=1.0 / D),
                         reads=[bankB[6]], writes=[bACC])
                    P.op("dve", lambda e, c0=c0, n=n: e.tensor_tensor(out=rl[:, c0:c0 + n], in0=mu[:, c0:c0 + n],
                                                                       in1=mu[:, c0:c0 + n], op=ALU.mult),
                         reads=[bACC], writes=[bRSTD])
                    P.op("dve", lambda e, c0=c0, n=n: e.scalar_tensor_tensor(
                        out=rl[:, c0:c0 + n], in0=banks[7][:, 0:n], scalar=1.0 / D, in1=rl[:, c0:c0 + n],
                        op0=ALU.mult, op1=ALU.subtract), reads=[bankB[7], bRSTD], writes=[bRSTD])
                    P.op("act", lambda e, c0=c0, n=n: e.activation(out=rl[:, c0:c0 + n], in_=rl[:, c0:c0 + n],
                                                                    func=AF.Sqrt, bias=pc_eps, scale=1.0),
                         reads=[bRSTD, bEPS], writes=[bRSTD])
                    P.op("dve", lambda e, c0=c0, n=n: e.reciprocal(out=rl[:, c0:c0 + n], in_=rl[:, c0:c0 + n]),
                         reads=[bRSTD], writes=[bRSTD])

                szz = [A.view(O_R4 + i * 4096, [OWN], F32) for i in range(2)]
                tt = [A.view(O_R4 + 8192 + i * 4096, [OWN], F32) for i in range(2)]
                bSZZ = [Buf("szz0"), Buf("szz1")]
                bTT = [Buf("tt0"), Buf("tt1")]
                phase(bSZZ + bTT, r4_all)
                zch = [(lambda kc, c0=c0, n=n: h1T[:, kc, HQ + c0:HQ + c0 + n], n) for (c0, n) in OCH]
                for f in range(KC):
                    i = f % 2
                    zb = [0, 1] if i == 0 else [2, 3]
                    wi = load_w([(slice(0, 128), w_in1[:, :, 8192 + 128 * f: 8192 + 128 * f + 128])])
                    formA(wi, zch, zb, r1b)
                    for c, (t0, n) in enumerate(OCH):
                        P.op("act", lambda e, c=c, t0=t0, n=n, i=i, f=f, zb=zb: e.activation(
                            out=szz[i][:, t0:t0 + n], in_=banks[zb[c]][:, 0:n], func=AF.Silu, bias=pc("bz1", f)),
                            reads=[bankB[zb[c]], bPRM], writes=[bSZZ[i]])
                    P.op("dve", lambda e, i=i, f=f: e.tensor_tensor(out=tt[i], in0=cst[:, f, :], in1=mu, op=ALU.subtract),
                         reads=[bC[f], bACC], writes=[bTT[i]])
                    P.op("dve", lambda e, i=i: e.tensor_tensor(out=tt[i], in0=tt[i], in1=rl, op=ALU.mult),
                         reads=[bTT[i], bRSTD], writes=[bTT[i]])
                    P.op("act", lambda e, i=i, f=f: e.activation(out=tt[i], in_=tt[i], func=AF.Silu,
                                                                 bias=pc("lnb", f), scale=pc("lng", f)),
                         reads=[bTT[i], bPRM], writes=[bTT[i]])
                    P.op("dve", lambda e, i=i, f=f: e.tensor_tensor(out=cst[:, f, :], in0=tt[i], in1=szz[i], op=ALU.mult),
                         reads=[bTT[i], bSZZ[i]], writes=[bC[f]])

                xr1 = [A.view(O_R4 + i * 4096, [OWN], F32) for i in range(2)]
                x2s = [A.view(O_R4 + 8192 + i * 4096, [OWN], F32) for i in range(2)]
                ys = [A.view(O_R4 + 16384 + i * 4096, [OWN], F32) for i in range(2)]
                sq2 = A.view(O_R4 + 24576, [OWN], F32)
                bXR1 = [Buf("xr1_0"), Buf("xr1_1")]
                bX2 = [Buf("x2_0"), Buf("x2_1")]
                bYS = [Buf("ys0"), Buf("ys1")]
                bSQ2 = Buf("sq2")
                phase(bXR1 + bX2 + bYS + [bSQ2], r4_all)
                sXR1 = sLD[0:2]
                sYS = sSTO
                bYD = [Buf("yd%d" % i) for i in range(KC)]
                yv = y_d.ap()
                cch = [(lambda kc, c0=c0, n=n: cst[:, kc, c0:c0 + n], n) for (c0, n) in OCH]
                a2 = acc[:, 0:OWN]
                for ob in range(KC):
                    i = ob % 2
                    zb = [0, 1] if i == 0 else [2, 3]
                    P.dma("sp", sXR1[i], xr1[i], x1v[ob][:, HQ:NQ], reads=[bX1D[ob]], writes=[bXR1[i]])
                    wi = load_w([(slice(0, 128), w_out1[:, :, 128 * ob:128 * ob + 128])])
                    formA(wi, cch, zb, lambda kc: [bC[kc]])
                    for c, (t0, n) in enumerate(OCH):
                        P.op("dve", lambda e, c=c, t0=t0, n=n, i=i, ob=ob, zb=zb: e.scalar_tensor_tensor(
                            out=x2s[i][:, t0:t0 + n], in0=banks[zb[c]][:, 0:n], scalar=pc("bo1", ob),
                            in1=xr1[i][:, t0:t0 + n], op0=ALU.add, op1=ALU.add),
                            reads=[bankB[zb[c]], bXR1[i], bPRM], writes=[bX2[i]])
                    if ob == 0:
                        P.op("dve", lambda e, i=i: e.tensor_tensor(out=a2, in0=x2s[i], in1=x2s[i], op=ALU.mult),
                             reads=[bX2[i]], writes=[bACC])
                    else:
                        P.op("act", lambda e, i=i: e.activation(out=sq2, in_=x2s[i], func=AF.Square),
                             reads=[bX2[i]], writes=[bSQ2])
                        P.op("dve", lambda e: e.tensor_tensor(out=a2, in0=a2, in1=sq2, op=ALU.add),
                             reads=[bSQ2, bACC], writes=[bACC])
                    P.op("act", lambda e, i=i, ob=ob: e.activation(out=ys[i], in_=x2s[i], func=AF.Identity,
                                                                   scale=pc("fg", ob)),
                         reads=[bX2[i], bPRM], writes=[bYS[i]])
                    P.dma("sp", sYS[i], yv[ob], ys[i], reads=[bYS[i]], writes=[bYD[ob]])
                rstd_from_sumsq(OWN, OCH)
                yi = [A.view(O_R4 + i * 4096, [OWN], F32) for i in range(2)]
                yo = [A.view(O_R4 + 8192 + i * 4096, [OWN], F32) for i in range(2)]
                bYI = [Buf("yi0"), Buf("yi1")]
                bYO = [Buf("yo0"), Buf("yo1")]
                phase(bYI + bYO, r4_all)
                sYI = sLD[0:2]
                sYO = sSTO
                for ob in range(KC):
                    i = ob % 2
                    P.dma("sp", sYI[i], yi[i], yv[ob], reads=[bYD[ob]], writes=[bYI[i]])
                    P.op("dve", lambda e, i=i: e.tensor_tensor(out=yo[i], in0=yi[i], in1=rstd[:, 0:OWN], op=ALU.mult),
                         reads=[bYI[i], bRSTD], writes=[bYO[i]])
                    P.dma("sp", sYO[i], yv[ob], yo[i], reads=[bYO[i], bYD[ob]], writes=[bYD[ob]])
                P.wait_all("sp", bYD)

        for ps in range(npass):
            emit_pass(ps)

        with nc.Block() as block:
            @block.sync
            def _(e):
                for f in P.q["sp"]:
                    f(e)

            @block.gpsimd
            def _(e):
                for f in P.q["pool"]:
                    f(e)

            @block.tensor
            def _(e):
                for f in P.q["pe"]:
                    f(e)

            @block.scalar
            def _(e):
                for f in P.q["act"]:
                    f(e)

            @block.vector
            def _(e):
                for f in P.q["dve"]:
                    f(e)
    return nc


def _pack_params(inp, j):
    prm = np.zeros((128, NPAR), np.float32)

    def put(name, arr):
        o0, n = POFF[name]
        assert arr.shape == (128, n), (name, arr.shape, n)
        prm[:, o0:o0 + n] = arr
    put("g0", _pcol(inp["norm_g"][0]))
    put("g1", _pcol(inp["norm_g"][1]))
    put("fg", _pcol(inp["final_g"]))
    b = np.asarray(inp["attn_b_in"][0], np.float32)
    put("bq", _pcol(b[0:4096]))
    bk = b[4096:4608].reshape(8, 64).T
    put("bk", np.concatenate([bk, bk], axis=0))
    put("bz", _pcol(b[5120:9216]))
    put("bv", np.broadcast_to(b[4608:5120], (128, 512)))
    put("bo0", _pcol(inp["attn_b_out"][0]))
    b1 = np.asarray(inp["conv_b_in"][0], np.float32)
    put("ba", _pcol(b1[0:4096]))
    put("bg", _pcol(b1[4096:8192]))
    put("bz1", _pcol(b1[8192:12288]))
    dw = np.asarray(inp["conv_dw_w"][0], np.float32)
    put("dww", np.ascontiguousarray(dw.T.reshape(32, 128, CW).transpose(1, 0, 2)).reshape(128, 32 * CW))
    put("dwb", _pcol(inp["conv_dw_b"][0]))
    put("lng", _pcol(inp["conv_ln_g"][0]))
    put("lnb", _pcol(inp["conv_ln_b"][0]))
    put("bo1", _pcol(inp["conv_b_out"][0]))
    s = np.asarray(inp["attn_sinks"][0], np.float32).reshape(8, 4, 2)
    sp = np.concatenate([s[:, :, 0], s[:, :, 1]], axis=1).reshape(64)
    put("sinks", np.broadcast_to(sp, (128, 64)))
    put("flag", np.full((128, 1), 1.0 if j > 0 else 0.0, np.float32))
    return prm


def _masks(j):
    s = np.arange(128)[:, None]
    q = np.arange(128)[None, :]
    cur = (s <= q).astype(np.float32)
    prev = (s > q).astype(np.float32)
    first = prev * (1.0 if j > 0 else 0.0)
    hp = np.zeros((128, 128), np.float32)
    hp[0:32, 0:32] = (s[0:32] > q[:, 0:32]).astype(np.float32)
    return np.ascontiguousarray(np.stack([cur, prev, first, hp], axis=1)).astype(ml_dtypes.bfloat16)


def _x_shard(x, c):
    b, j = c // 4, c % 4
    s = j * OWN
    xs = np.zeros((NK, D), np.float32)
    lo = s - HK
    if lo < 0:
        xs[-lo:] = x[b, 0:s + OWN]
    else:
        xs[:] = x[b, lo:s + OWN]
    return np.ascontiguousarray(xs.T).reshape(KC, 128, NK)


_NC_CACHE = {}

NCU = 8
NPASS = NCORES // NCU


def _get_nc(mode, npass):
    key = (mode, npass)
    if key not in _NC_CACHE:
        _NC_CACHE[key] = build(mode, npass)
    return _NC_CACHE[key]


def kernel(**inputs):
    inp = {k: np.asarray(v) for k, v in inputs.items()}
    x = np.asarray(inp["x"], np.float32)
    w_in0 = np.ascontiguousarray(inp["attn_w_in"][0], dtype=np.float32)
    w_out0 = np.ascontiguousarray(inp["attn_w_out"][0], dtype=np.float32)
    w_in1 = np.ascontiguousarray(inp["conv_w_in"][0], dtype=np.float32)
    w_out1 = np.ascontiguousarray(inp["conv_w_out"][0], dtype=np.float32)
    prm = [_pack_params(inp, j) for j in range(2)]
    msk = [_masks(j) for j in range(2)]
    in_maps = []
    for c in range(NCU):
        m = {"attn_w_in": w_in0, "attn_w_out": w_out0, "conv_w_in": w_in1, "conv_w_out": w_out1}
        for ps in range(NPASS):
            v = c * NPASS + ps
            sfx = "" if NPASS == 1 else "_%d" % ps
            jj = min(v % 4, 1)
            m["params" + sfx] = prm[jj]
            m["masks" + sfx] = msk[jj]
            m["xT" + sfx] = _x_shard(x, v)
        in_maps.append(m)
    res = run_bass_kernel_spmd(_get_nc("fused", NPASS), in_maps, core_ids=list(range(NCU)))
    out = np.empty((2, 4096, D), np.float32)
    for c in range(NCU):
        for ps in range(NPASS):
            v = c * NPASS + ps
            b, j = v // 4, v % 4
            sfx = "" if NPASS == 1 else "_%d" % ps
            yT = np.asarray(res.results[c]["yT" + sfx], np.float32).reshape(D, OWN)
            out[b, j * OWN:(j + 1) * OWN, :] = yT.T
    return out
```
